# Optimizing a Trainium2 kernel written in Bass

```python
import math
import jax, jax.numpy as jnp
from jax import lax
import numpy as np

D_MODEL = 1024
BATCH = 8
SEQ = 4096
DEPTH = 4

PLE_DIM = 256
HEAD_DIM = 64
LRU_WIDTH = D_MODEL // 2
LRU_BLOCKS = 8
LRU_BLOCK = LRU_WIDTH // LRU_BLOCKS
CONV_WIDTH = 4
LRU_C = 8.0
SB_HEADS = 8
SB_WIDTH = SB_HEADS * HEAD_DIM
DIFF_HEADS = 8
DIFF_QK = DIFF_HEADS * 2 * HEAD_DIM
DIFF_V = DIFF_HEADS * 2 * HEAD_DIM
EVEN_IN = 2 * LRU_WIDTH + 4 * SB_WIDTH
EVEN_MIX = LRU_WIDTH + SB_WIDTH
ODD_IN = 2 * DIFF_QK + 2 * DIFF_V
ODD_MIX = DIFF_V
Q_BLOCK = 128
ROPE_THETA = 10000.0
EPS = 1e-6

kernel_name = 'hybrid_rglru_stickbreak_diffattn_trunk'


def rmsnorm(x, g):
    xf = x.astype(jnp.float32)
    ms = jnp.mean(xf * xf, axis=-1, keepdims=True)
    return (xf * lax.rsqrt(ms + EPS) * g.astype(jnp.float32)).astype(x.dtype)


def rope(x, positions):
    dh = x.shape[-1]
    inv_freq = ROPE_THETA ** (-jnp.arange(0, dh, 2, dtype=jnp.float32) / dh)
    ang = positions.astype(jnp.float32)[..., None] * inv_freq
    cos = jnp.cos(ang)[:, :, None, :]
    sin = jnp.sin(ang)[:, :, None, :]
    xf = x.astype(jnp.float32)
    x1, x2 = xf[..., : dh // 2], xf[..., dh // 2:]
    out = jnp.concatenate([x1 * cos - x2 * sin, x2 * cos + x1 * sin], axis=-1)
    return out.astype(x.dtype)


def causal_depthwise_conv(x, w, b):
    S = x.shape[1]
    xp = jnp.pad(x, ((0, 0), (CONV_WIDTH - 1, 0), (0, 0)))
    out = b
    for k in range(CONV_WIDTH):
        out = out + w[k] * xp[:, k:k + S]
    return out


def block_diag_linear(x, w, b):
    B, S, C = x.shape
    xb = x.reshape(B, S, LRU_BLOCKS, LRU_BLOCK)
    return jnp.einsum('bsnc,ncd->bsnd', xb, w).reshape(B, S, C) + b


def rg_lru(x, w_a, b_a, w_x, b_x, lam):
    r = jax.nn.sigmoid(block_diag_linear(x, w_a, b_a).astype(jnp.float32))
    i = jax.nn.sigmoid(block_diag_linear(x, w_x, b_x).astype(jnp.float32))
    log_a = LRU_C * r * jax.nn.log_sigmoid(lam.astype(jnp.float32))
    a = jnp.exp(log_a)
    u = jnp.sqrt(-jnp.expm1(2.0 * log_a)) * (i * x.astype(jnp.float32))

    def combine(left, right):
        a_l, b_l = left
        a_r, b_r = right
        return a_l * a_r, a_r * b_l + b_r

    _, h = lax.associative_scan(combine, (a, u), axis=1)
    return h.astype(x.dtype)


def stick_breaking_attention(q, k, v):
    B, S, H, Dh = q.shape
    scale = Dh ** -0.5
    outs = []
    for start in range(0, S, Q_BLOCK):
        end = start + Q_BLOCK
        z = jnp.einsum('bqhd,bkhd->bhqk', q[:, start:end].astype(jnp.float32),
                       k[:, :end].astype(jnp.float32)) * scale
        mask = jnp.arange(end)[None, :] < (start + jnp.arange(Q_BLOCK))[:, None]
        log_1m = jnp.where(mask, jax.nn.log_sigmoid(-z), 0.0)
        cs = jnp.cumsum(log_1m, axis=-1)
        log_w = jax.nn.log_sigmoid(z) + (cs[..., -1:] - cs)
        w = jnp.where(mask, jnp.exp(log_w), 0.0)
        outs.append(jnp.einsum('bhqk,bkhd->bqhd', w.astype(v.dtype), v[:, :end]))
    return jnp.concatenate(outs, axis=1)


def _causal_softmax_block(q_blk, k_pre, mask, scale):
    s = jnp.einsum('bqhd,bkhd->bhqk', q_blk.astype(jnp.float32), k_pre.astype(jnp.float32)) * scale
    s = jnp.where(mask, s, -jnp.inf)
    return jax.nn.softmax(s, axis=-1)


def differential_attention(q1, q2, k1, k2, v, lam):
    B, S, H, Dh = q1.shape
    scale = Dh ** -0.5
    outs = []
    for start in range(0, S, Q_BLOCK):
        end = start + Q_BLOCK
        mask = jnp.arange(end)[None, :] <= (start + jnp.arange(Q_BLOCK))[:, None]
        p1 = _causal_softmax_block(q1[:, start:end], k1[:, :end], mask, scale)
        p2 = _causal_softmax_block(q2[:, start:end], k2[:, :end], mask, scale)
        w = p1 - lam.astype(jnp.float32) * p2
        outs.append(jnp.einsum('bhqk,bkhd->bqhd', w.astype(v.dtype), v[:, :end]))
    return jnp.concatenate(outs, axis=1)


def even_mixer(h, w_in, conv_w, conv_b, w_a, b_a, w_x, b_x, lam, w_out):
    B, S, _ = h.shape
    y = h @ w_in
    xa, ga, q, k, v, gb = jnp.split(
        y, [LRU_WIDTH, 2 * LRU_WIDTH, 2 * LRU_WIDTH + SB_WIDTH,
            2 * LRU_WIDTH + 2 * SB_WIDTH, 2 * LRU_WIDTH + 3 * SB_WIDTH], axis=-1)
    xa = causal_depthwise_conv(xa, conv_w, conv_b)
    oa = rg_lru(xa, w_a, b_a, w_x, b_x, lam) * jax.nn.silu(ga)
    q = q.reshape(B, S, SB_HEADS, HEAD_DIM)
    k = k.reshape(B, S, SB_HEADS, HEAD_DIM)
    v = v.reshape(B, S, SB_HEADS, HEAD_DIM)
    ob = stick_breaking_attention(q, k, v).reshape(B, S, SB_WIDTH) * jax.nn.silu(gb)
    return jnp.concatenate([oa, ob], axis=-1) @ w_out


def odd_mixer(h, positions, w_in, lq1, lk1, lq2, lk2, subln_g, w_out, lambda_init):
    B, S, _ = h.shape
    y = h @ w_in
    q, k, v, g = jnp.split(y, [DIFF_QK, 2 * DIFF_QK, 2 * DIFF_QK + DIFF_V], axis=-1)
    q = rope(q.reshape(B, S, 2 * DIFF_HEADS, HEAD_DIM), positions).reshape(B, S, DIFF_HEADS, 2, HEAD_DIM)
    k = rope(k.reshape(B, S, 2 * DIFF_HEADS, HEAD_DIM), positions).reshape(B, S, DIFF_HEADS, 2, HEAD_DIM)
    v = v.reshape(B, S, DIFF_HEADS, 2 * HEAD_DIM)
    lam = (jnp.exp(jnp.sum(lq1.astype(jnp.float32) * lk1.astype(jnp.float32)))
           - jnp.exp(jnp.sum(lq2.astype(jnp.float32) * lk2.astype(jnp.float32))) + lambda_init)
    o = differential_attention(q[:, :, :, 0], q[:, :, :, 1], k[:, :, :, 0], k[:, :, :, 1], v, lam)
    o = rmsnorm(o, subln_g) * (1.0 - lambda_init)
    o = o.reshape(B, S, ODD_MIX) * jax.nn.silu(g)
    return o @ w_out


def setup_inputs(seed: int = 0) -> dict:
    key = jax.random.key(seed)
    n_even = (DEPTH + 1) // 2
    n_odd = DEPTH // 2
    ks = jax.random.split(key, 24)

    def nrm(k, shape, scale):
        return scale * jax.random.normal(k, shape, jnp.float32)

    x = nrm(ks[0], (BATCH, SEQ, D_MODEL), 1.0)
    p = nrm(ks[1], (DEPTH, BATCH, SEQ, PLE_DIM), 1.0)
    positions = jnp.broadcast_to(jnp.arange(SEQ, dtype=jnp.int32), (BATCH, SEQ))
    norm_mix = 1.0 + nrm(ks[2], (DEPTH, D_MODEL), 0.02)
    norm_ple = 1.0 + nrm(ks[3], (DEPTH, D_MODEL), 0.02)
    w_ple_gate = nrm(ks[4], (DEPTH, D_MODEL, D_MODEL), D_MODEL ** -0.5)
    w_ple_proj = nrm(ks[5], (DEPTH, PLE_DIM, D_MODEL), 0.5 * PLE_DIM ** -0.5)
    w_in_e = nrm(ks[6], (n_even, D_MODEL, EVEN_IN), D_MODEL ** -0.5)
    conv_w = nrm(ks[7], (n_even, CONV_WIDTH, LRU_WIDTH), CONV_WIDTH ** -0.5)
    conv_b = nrm(ks[8], (n_even, LRU_WIDTH), 0.02)
    lru_wa = nrm(ks[9], (n_even, LRU_BLOCKS, LRU_BLOCK, LRU_BLOCK), LRU_BLOCK ** -0.5)
    lru_ba = nrm(ks[10], (n_even, LRU_WIDTH), 0.1)
    lru_wx = nrm(ks[11], (n_even, LRU_BLOCKS, LRU_BLOCK, LRU_BLOCK), LRU_BLOCK ** -0.5)
    lru_bx = nrm(ks[12], (n_even, LRU_WIDTH), 0.1)
    u = jax.random.uniform(ks[13], (n_even, LRU_WIDTH), jnp.float32, minval=0.9, maxval=0.999)
    a0 = u ** (1.0 / LRU_C)
    lru_lambda = jnp.log(a0) - jnp.log1p(-a0)
    w_out_e = nrm(ks[14], (n_even, EVEN_MIX, D_MODEL), EVEN_MIX ** -0.5)
    w_in_o = nrm(ks[15], (n_odd, D_MODEL, ODD_IN), D_MODEL ** -0.5)
    lam_q1 = nrm(ks[16], (n_odd, HEAD_DIM), 0.1)
    lam_k1 = nrm(ks[17], (n_odd, HEAD_DIM), 0.1)
    lam_q2 = nrm(ks[18], (n_odd, HEAD_DIM), 0.1)
    lam_k2 = nrm(ks[19], (n_odd, HEAD_DIM), 0.1)
    subln_g = 1.0 + nrm(ks[20], (n_odd, 2 * HEAD_DIM), 0.02)
    w_out_o = nrm(ks[21], (n_odd, ODD_MIX, D_MODEL), ODD_MIX ** -0.5)
    final_norm = 1.0 + nrm(ks[22], (D_MODEL,), 0.02)
    return {'x': x, 'p': p, 'positions': positions, 'norm_mix': norm_mix, 'norm_ple': norm_ple,
            'w_ple_gate': w_ple_gate, 'w_ple_proj': w_ple_proj, 'w_in_e': w_in_e,
            'conv_w': conv_w, 'conv_b': conv_b, 'lru_wa': lru_wa, 'lru_ba': lru_ba,
            'lru_wx': lru_wx, 'lru_bx': lru_bx, 'lru_lambda': lru_lambda, 'w_out_e': w_out_e,
            'w_in_o': w_in_o, 'lam_q1': lam_q1, 'lam_k1': lam_k1, 'lam_q2': lam_q2,
            'lam_k2': lam_k2, 'subln_g': subln_g, 'w_out_o': w_out_o, 'final_norm': final_norm}


def reference(x, p, positions, norm_mix, norm_ple, w_ple_gate, w_ple_proj, w_in_e,
              conv_w, conv_b, lru_wa, lru_ba, lru_wx, lru_bx, lru_lambda, w_out_e,
              w_in_o, lam_q1, lam_k1, lam_q2, lam_k2, subln_g, w_out_o, final_norm):
    h = x
    for i in range(DEPTH):
        j = i // 2
        hn = rmsnorm(h, norm_mix[i])
        if i % 2 == 0:
            mix = even_mixer(hn, w_in_e[j], conv_w[j], conv_b[j], lru_wa[j], lru_ba[j],
                             lru_wx[j], lru_bx[j], lru_lambda[j], w_out_e[j])
        else:
            lambda_init = 0.8 - 0.6 * math.exp(-0.3 * i)
            mix = odd_mixer(hn, positions, w_in_o[j], lam_q1[j], lam_k1[j], lam_q2[j],
                            lam_k2[j], subln_g[j], w_out_o[j], lambda_init)
        h = h + mix
        gate = jax.nn.sigmoid(rmsnorm(h, norm_ple[i]) @ w_ple_gate[i])
        h = h + gate * (p[i] @ w_ple_proj[i])
    return rmsnorm(h, final_norm)
```

```python
import math
from contextlib import ExitStack

import numpy as np
import concourse.bass as bass
import concourse.mybir as mybir
from concourse.bass_utils import run_bass_kernel_spmd

F32 = mybir.dt.float32
BF16 = mybir.dt.bfloat16
I32 = mybir.dt.int32
AF = mybir.ActivationFunctionType
ALU = mybir.AluOpType

S = 4096
D = 1024
NT = S // 128
DEPTH = 4
EPS = 1e-6
NEG = -30000.0
TWO_PI = 2.0 * math.pi
CW1 = 6.28125
CW2 = TWO_PI - CW1


class Sem:
    def __init__(self, handle, key):
        self.h = handle
        self.key = key
        self.total = 0


class Buf:
    def __init__(self, t, name):
        self.t = t
        self.name = name
        self.last_w = None
        self.reads = {}
        self.sem = None

    def __getitem__(self, idx):
        return self.t[idx]


class Kern:
    ENGS = ("pe", "act", "dve", "pool", "sp")

    def __init__(self, nc, es, n_dma_sems=40):
        self.nc = nc
        self.eng = {"pe": nc.tensor, "act": nc.scalar, "dve": nc.vector,
                    "pool": nc.gpsimd, "sp": nc.sync}
        self.csem = {e: Sem(es.enter_context(nc.semaphore("c_" + e)), "c_" + e) for e in self.ENGS}
        self.known = {e: {} for e in self.ENGS}
        self.free_sems = [Sem(es.enter_context(nc.semaphore("d%d" % i)), "d%d" % i)
                          for i in range(n_dma_sems)]
        self.live_sems = []
        self.n_instr = 0

    def _collect(self, e, reads, writes):
        waits = {}

        def need(ev, raw):
            if ev is None:
                return
            s, v, src = ev
            if src == e and e != "sp" and not raw:
                return
            if src == e and e == "pe":
                return
            if self.known[e].get(s.key, 0) >= v:
                return
            if s.key not in waits or waits[s.key][1] < v:
                waits[s.key] = (s, v)

        for b in reads:
            need(b.last_w, True)
        for b in writes:
            need(b.last_w, False)
            for ev in b.reads.values():
                need(ev, False)
        return list(waits.values())

    def _emit(self, e, fn, waits):
        eng = self.eng[e]
        for s, v in waits[:-1]:
            eng.wait_ge(s.h, v)
            self.known[e][s.key] = v
        ins = fn(eng)
        if waits:
            s, v = waits[-1]
            ins._wait_ge(s.h, v)
            self.known[e][s.key] = v
        self.n_instr += 1
        return ins

    def op(self, e, fn, reads=(), writes=(), sig=True):
        waits = self._collect(e, reads, writes)
        ins = self._emit(e, fn, waits)
        cs = self.csem[e]
        if sig:
            cs.total += 1
            ins.then_inc(cs.h, 1)
            ev = (cs, cs.total, e)
        else:
            ev = (cs, cs.total + 1, e)
        for b in writes:
            b.last_w = ev
            b.reads = {}
        for b in reads:
            if b not in writes:
                b.reads[cs.key] = ev
        return ins

    def dma(self, out, in_, buf, load, q="sp"):
        if buf.sem is None:
            buf.sem = self.free_sems.pop()
            self.live_sems.append(buf.sem)
        s = buf.sem
        waits = {}
        evs = []
        if load:
            if buf.last_w is not None and buf.last_w[0] is not s:
                evs.append(buf.last_w)
            evs.extend(ev for ev in buf.reads.values())
        else:
            if buf.last_w is not None:
                evs.append(buf.last_w)
        for ev in evs:
            ss, v, _ = ev
            if self.known[q].get(ss.key, 0) >= v:
                continue
            if ss.key not in waits or waits[ss.key][1] < v:
                waits[ss.key] = (ss, v)
        ins = self._emit(q, lambda eng: eng.dma_start(out=out, in_=in_), list(waits.values()))
        s.total += 16
        ins.then_inc(s.h, 16)
        ev = (s, s.total, None)
        if load:
            buf.last_w = ev
            buf.reads = {}
        else:
            buf.reads[s.key] = ev
        return ins

    def barrier(self):
        sp = self.eng["sp"]
        for s in self.live_sems:
            if self.known["sp"].get(s.key, 0) < s.total:
                sp.wait_ge(s.h, s.total)
                self.known["sp"][s.key] = s.total
        for e in self.ENGS:
            if e == "sp":
                continue
            cs = self.csem[e]
            if self.known["sp"].get(cs.key, 0) < cs.total:
                sp.wait_ge(cs.h, cs.total)
                self.known["sp"][cs.key] = cs.total
        cs = self.csem["sp"]
        cs.total += 1
        sp.sem_inc(cs.h, 1)
        for e in self.ENGS:
            if e == "sp":
                continue
            self.eng[e].wait_ge(cs.h, cs.total)
            self.known[e][cs.key] = cs.total
            for s in self.live_sems:
                self.known[e][s.key] = s.total
            for e2 in self.ENGS:
                self.known[e][self.csem[e2].key] = self.csem[e2].total
        for e2 in self.ENGS:
            self.known["sp"][self.csem[e2].key] = self.csem[e2].total
        self.free_sems.extend(self.live_sems)
        self.live_sems = []


class Prog:
    def __init__(self, n_layers=DEPTH, dbg=False):
        self.n_layers = n_layers
        self.dbg = dbg
        self.nc = bass.Bass("TRN2", target_bir_lowering=False)

    def din(self, name, shape, dt=F32):
        return self.nc.dram_tensor(name, list(shape), dt, kind="ExternalInput").ap()

    def dscr(self, name, shape, dt=F32):
        kind = "ExternalOutput" if self.dbg else "Internal"
        return self.nc.dram_tensor(name, list(shape), dt, kind=kind).ap()

    def sb(self, es, name, shape, dt=F32):
        self._uid = getattr(self, "_uid", 0) + 1
        name = "%s_%d" % (name, self._uid)
        t = es.enter_context(self.nc.sbuf_tensor(name, list(shape), dt))
        return Buf(t, name)

    def build(self):
        nc = self.nc
        self.x = self.din("x", [S, D])
        self.p = self.din("p", [DEPTH, S, 256])
        self.pos = self.din("pos", [1, S], I32)
        self.norm_mix = self.din("norm_mix", [DEPTH, D])
        self.norm_ple = self.din("norm_ple", [DEPTH, D])
        self.w_ple_gate = self.din("w_ple_gate", [DEPTH, D, D])
        self.w_ple_proj = self.din("w_ple_proj", [DEPTH, 256, D])
        self.w_in_e = self.din("w_in_e", [2, D, 3072])
        self.pvec_e = self.din("pvec_e", [2, 128, 4, 8])
        self.lru_wa = self.din("lru_wa", [2, 8, 64, 64])
        self.lru_wx = self.din("lru_wx", [2, 8, 64, 64])
        self.w_out_e = self.din("w_out_e", [2, D, D])
        self.w_in_o = self.din("w_in_o", [2, D, 4096])
        self.lamv = self.din("lamv", [2, 1, 256])
        self.subg = self.din("subg", [128, 2])
        self.w_out_o = self.din("w_out_o", [2, D, D])
        self.final_norm = self.din("final_norm", [1, D])
        self.c_ident = self.din("c_ident", [128, 128])
        self.c_ntinc = self.din("c_ntinc", [128, 128])
        self.c_mask_s = self.din("c_mask_s", [128, 4, 512])
        self.c_mask_i = self.din("c_mask_i", [128, 2, 256])
        self.c_freq = self.din("c_freq", [128, 2])
        self.c_perm = self.din("c_perm", [128, 128])
        self.out = nc.dram_tensor("out", [S, D], F32, kind="ExternalOutput").ap()

        self.h = self.dscr("h_scr", [S, D])
        self.qT = self.dscr("qT_scr", [D, S], BF16)
        self.kT = self.dscr("kT_scr", [D, S], BF16)
        self.v = self.dscr("v_scr", [S, D], BF16)
        self.g1T = self.dscr("g1T_scr", [D, S])
        self.gtok = self.dscr("gtok_scr", [S, D])
        self.xaT = self.dscr("xaT_scr", [512, S])
        self.mixT = self.dscr("mixT_scr", [D, S], BF16)
        self.mixtok = self.dscr("mixtok_scr", [S, D], BF16)
        self.rope = self.dscr("rope_scr", [4, 128, S])

        with ExitStack() as es:
            self.K = Kern(nc, es)
            K = self.K
            es.enter_context(nc.Block())
            pq = [es.enter_context(nc.psum_tensor("pq%d" % i, [128, 1024], F32)) for i in range(4)]
            self.ps = [Buf(pq[i // 2][:, (i % 2) * 512:(i % 2) * 512 + 512], "ps%d" % i) for i in range(6)]
            self.pb = [Buf(pq[3][:, i * 512:(i + 1) * 512].bitcast(BF16), "pb%d" % i) for i in range(2)]
            self.pq2 = [Buf(pq[i], "pq%d" % i) for i in range(3)]
            self.po = [Buf(pq[3][:, i * 512:(i + 1) * 512], "po%d" % i) for i in range(2)]
            self.ident = self.sb(es, "ident", [128, 128], BF16)
            self.consts_phase(es)
            self.rope_phase()
            for li in range(self.n_layers):
                src = self.x if li == 0 else self.h
                if li % 2 == 0:
                    self.inproj_even(li, src)
                    self.lru_phase(li)
                    self.sb_phase(li)
                else:
                    self.inproj_odd(li, src)
                    self.diff_phase(li)
                self.out_phase(li, src)
            self.final_phase(self.x if self.n_layers == 0 else self.h)
        return nc

    def rms_scale(self, ss, rstd, n, nfeat):
        K = self.K
        K.op("dve", lambda e: e.tensor_scalar(out=rstd[:, 0:n], in0=ss[:, 0:n], scalar1=1.0 / nfeat,
                                              scalar2=EPS, op0=ALU.mult, op1=ALU.add),
             reads=[ss], writes=[rstd])
        K.op("act", lambda e: e.activation(out=rstd[:, 0:n], in_=rstd[:, 0:n], func=AF.Ln),
             reads=[rstd], writes=[rstd])
        K.op("act", lambda e: e.activation(out=rstd[:, 0:n], in_=rstd[:, 0:n], func=AF.Exp, scale=-0.5),
             reads=[rstd], writes=[rstd])

    def consts_phase(self, es):
        K = self.K
        with ExitStack() as ts:
            stg = self.sb(ts, "cst_stg", [128, 128])
            K.dma(stg[:], self.c_ident, stg, True)
            K.op("dve", lambda e: e.tensor_copy(out=self.ident[:], in_=stg[:]), reads=[stg], writes=[self.ident])
            K.barrier()

    def rope_phase(self):
        K = self.K
        CH = S // 4
        with ExitStack() as ts:
            posi = self.sb(ts, "posi", [128, CH], I32)
            ang = self.sb(ts, "ang", [128, CH])
            kf = self.sb(ts, "kf", [128, CH])
            ki = self.sb(ts, "ki", [128, CH], I32)
            r = self.sb(ts, "r", [128, CH])
            t = self.sb(ts, "t", [128, CH])
            y = self.sb(ts, "y", [128, CH])
            fr = self.sb(ts, "fr", [128, 2])
            for c in range(4):
                K.dma(posi[c * 32:(c + 1) * 32, :], self.pos[0:1, c * CH:(c + 1) * CH].broadcast_to([32, CH]), posi, True)
            K.dma(fr[:], self.c_freq, fr, True)
            K.op("dve", lambda e: e.tensor_copy(out=ang[:], in_=posi[:]), reads=[posi], writes=[ang])
            K.op("dve", lambda e: e.tensor_scalar(out=ang[:], in0=ang[:], scalar1=fr[:, 0:1], scalar2=None,
                                                  op0=ALU.mult), reads=[ang, fr], writes=[ang])
            K.op("dve", lambda e: e.tensor_scalar(out=kf[:], in0=ang[:], scalar1=1.0 / TWO_PI, scalar2=None,
                                                  op0=ALU.mult), reads=[ang], writes=[kf])
            K.op("dve", lambda e: e.tensor_copy(out=ki[:], in_=kf[:]), reads=[kf], writes=[ki])
            K.op("dve", lambda e: e.tensor_copy(out=kf[:], in_=ki[:]), reads=[ki], writes=[kf])
            K.op("dve", lambda e: e.scalar_tensor_tensor(out=r[:], in0=kf[:], scalar=-CW1, in1=ang[:],
                                                         op0=ALU.mult, op1=ALU.add), reads=[kf, ang], writes=[r])
            K.op("dve", lambda e: e.scalar_tensor_tensor(out=r[:], in0=kf[:], scalar=-CW2, in1=r[:],
                                                         op0=ALU.mult, op1=ALU.add), reads=[kf, r], writes=[r])

            def wrap(dst, src, shift):
                K.op("dve", lambda e: e.tensor_scalar(out=dst[:], in0=src[:], scalar1=shift, scalar2=None,
                                                      op0=ALU.add), reads=[src], writes=[dst])
                K.op("dve", lambda e: e.tensor_scalar(out=t[:], in0=dst[:], scalar1=math.pi, scalar2=-TWO_PI,
                                                      op0=ALU.is_gt, op1=ALU.mult), reads=[dst], writes=[t])
                K.op("dve", lambda e: e.tensor_tensor(out=dst[:], in0=dst[:], in1=t[:], op=ALU.add),
                     reads=[dst, t], writes=[dst])
                K.op("dve", lambda e: e.tensor_scalar(out=t[:], in0=dst[:], scalar1=-math.pi, scalar2=TWO_PI,
                                                      op0=ALU.is_lt, op1=ALU.mult), reads=[dst], writes=[t])
                K.op("dve", lambda e: e.tensor_tensor(out=dst[:], in0=dst[:], in1=t[:], op=ALU.add),
                     reads=[dst, t], writes=[dst])
                K.op("dve", lambda e: e.tensor_scalar(out=dst[:], in0=dst[:], scalar1=-3.1415925, scalar2=3.1415925,
                                                      op0=ALU.max, op1=ALU.min), reads=[dst], writes=[dst])

            def emit(tile, idx, reps):
                for rep in reps:
                    for c in range(4):
                        K.dma(self.rope[idx, rep * 32:(rep + 1) * 32, c * CH:(c + 1) * CH], tile[c * 32:(c + 1) * 32, :],
                              tile, False)

            for which, shift in ((1, 0.0), (0, math.pi / 2)):
                o = self.sb(ts, "ro%d" % which, [128, CH])
                o8 = self.sb(ts, "ro8_%d" % which, [128, CH])
                wrap(y, r, shift)
                K.op("act", lambda e: e.activation(out=o[:], in_=y[:], func=AF.Sin), reads=[y], writes=[o])
                K.op("pool", lambda e: e.tensor_scalar(out=o8[:], in0=o[:], scalar1=0.125, scalar2=None,
                                                       op0=ALU.mult), reads=[o], writes=[o8])
                if which == 1:
                    on_ = self.sb(ts, "ron", [128, CH])
                    on8 = self.sb(ts, "ron8", [128, CH])
                    K.op("pool", lambda e: e.tensor_scalar(out=on_[:], in0=o[:], scalar1=-1.0, scalar2=None,
                                                           op0=ALU.mult), reads=[o], writes=[on_])
                    K.op("pool", lambda e: e.tensor_scalar(out=on8[:], in0=o[:], scalar1=-0.125, scalar2=None,
                                                           op0=ALU.mult), reads=[o], writes=[on8])
                    emit(o, 3, (1, 3))
                    emit(on_, 3, (0, 2))
                    emit(o8, 1, (1, 3))
                    emit(on8, 1, (0, 2))
                else:
                    emit(o, 2, (0, 1, 2, 3))
                    emit(o8, 0, (0, 1, 2, 3))
            K.barrier()

    def inproj(self, src, gain_row, wsrc, col0, ncols, fm_specs, tm_specs, rope_tabs=False, perm=False):
        K = self.K
        with ExitStack() as ts:
            w = self.sb(ts, "w_in", [128, 8, ncols], BF16)
            nstg = 5
            stgs = [self.sb(ts, "w_stg%d" % i, [128, 1024]) for i in range(nstg)]
            permT = None
            if perm:
                permT = self.sb(ts, "permT", [128, 128], BF16)
                K.dma(stgs[0][:, 0:128], self.c_perm, stgs[0], True)
                K.op("dve", lambda e: e.tensor_copy(out=permT[:], in_=stgs[0][:, 0:128]), reads=[stgs[0]], writes=[permT])
            n = 0
            for kc in range(8):
                for c0 in range(0, ncols, 1024):
                    cw = min(1024, ncols - c0)
                    stg = stgs[n % nstg]
                    n += 1
                    K.dma(stg[:, 0:cw], wsrc[kc * 128:(kc + 1) * 128, col0 + c0:col0 + c0 + cw], stg, True)
                    if n % 2 == 0:
                        K.op("act", lambda e: e.activation(out=w[:, kc, c0:c0 + cw], in_=stg[:, 0:cw], func=AF.Copy),
                             reads=[stg], writes=[w])
                    else:
                        K.op("dve", lambda e: e.tensor_copy(out=w[:, kc, c0:c0 + cw], in_=stg[:, 0:cw]),
                             reads=[stg], writes=[w])
            gbc = self.sb(ts, "gbc", [128, D])
            K.dma(gbc[:], gain_row.broadcast_to([128, D]), gbc, True)
            hx = [self.sb(ts, "hx%d" % i, [128, 4, D]) for i in range(2)]
            hn = self.sb(ts, "hn", [128, 4, D], BF16)
            hnT = [self.sb(ts, "hnT%d" % i, [128, 8, 512], BF16) for i in range(2)]
            junk = self.sb(ts, "junk", [128, D])
            ss = [self.sb(ts, "ss%d" % i, [128, 4]) for i in range(2)]
            rstd = [self.sb(ts, "rstd%d" % i, [128, 4]) for i in range(2)]
            o32 = [self.sb(ts, "o32_%d" % i, [128, 512]) for i in range(3)]
            o16 = [self.sb(ts, "o16_%d" % i, [128, 512], BF16) for i in range(3)]
            if rope_tabs:
                rt = [self.sb(ts, "rt%d" % i, [128, 4, 512]) for i in range(2)]
                xbs = [self.sb(ts, "xb_%d" % i, [128, 512], BF16) for i in range(3)]
                t1 = [self.sb(ts, "t1_%d" % i, [128, 512]) for i in range(2)]
                t2 = [self.sb(ts, "t2_%d" % i, [128, 512]) for i in range(2)]
            cnt = {"ps": 0, "o32": 0, "o16": 0, "cp": 0, "xb": 0}

            def next_ps():
                b = self.ps[cnt["ps"] % 6]
                cnt["ps"] += 1
                return b

            def next_o32():
                b = o32[cnt["o32"] % 3]
                cnt["o32"] += 1
                return b

            def next_o16():
                b = o16[cnt["o16"] % 3]
                cnt["o16"] += 1
                return b

            def evac(dst, ps, kind):
                if kind == "silu":
                    K.op("act", lambda e: e.activation(out=dst[:], in_=ps[:], func=AF.Silu), reads=[ps], writes=[dst])
                elif kind == "scale8":
                    K.op("act", lambda e: e.activation(out=dst[:], in_=ps[:], func=AF.Copy, scale=0.125),
                         reads=[ps], writes=[dst])
                else:
                    cnt["cp"] += 1
                    if cnt["cp"] % 2 == 0:
                        K.op("dve", lambda e: e.tensor_copy(out=dst[:], in_=ps[:]), reads=[ps], writes=[dst])
                    else:
                        K.op("act", lambda e: e.activation(out=dst[:], in_=ps[:], func=AF.Copy),
                             reads=[ps], writes=[dst])

            hns = [hn, self.sb(ts, "hn_b", [128, 4, D], BF16)]

            def N1(tb):
                hxb = hx[tb % 2]
                ssb, rsb = ss[tb % 2], rstd[tb % 2]
                hnb = hns[tb % 2]
                K.dma(hxb[:], src[tb * 512:(tb + 1) * 512, :].rearrange("(j p) d -> p j d", p=128), hxb, True)
                for jj in range(4):
                    K.op("act", lambda e: e.activation(out=junk[:], in_=hxb[:, jj, :], func=AF.Square,
                                                       accum_out=ssb[:, jj:jj + 1]),
                         reads=[hxb], writes=[junk, ssb])
                self.rms_scale(ssb, rsb, 4, D)
                for jj in range(4):
                    K.op("dve", lambda e: e.scalar_tensor_tensor(out=hnb[:, jj, :], in0=hxb[:, jj, :],
                                                                 scalar=rsb[:, jj:jj + 1], in1=gbc[:],
                                                                 op0=ALU.mult, op1=ALU.mult),
                         reads=[hxb, rsb, gbc], writes=[hnb])

            def N2(tb):
                hT = hnT[tb % 2]
                hnb = hns[tb % 2]
                if rope_tabs:
                    rtb = rt[tb % 2]
                    K.dma(rtb[:], self.rope[:, :, tb * 512:(tb + 1) * 512].rearrange("f p t -> p f t"), rtb, True)
                for kc in range(8):
                    pb = self.pb[kc % 2]
                    for jj in range(4):
                        K.op("pe", lambda e: e.transpose(out=pb[:, jj * 128:(jj + 1) * 128],
                                                         in_=hnb[:, jj, kc * 128:(kc + 1) * 128],
                                                         identity=self.ident[:]),
                             reads=[hnb, self.ident], writes=[pb], sig=(jj == 3))
                    if kc % 2 == 0:
                        K.op("act", lambda e: e.activation(out=hT[:, kc, :], in_=pb[:, 0:512], func=AF.Copy),
                             reads=[pb], writes=[hT])
                    else:
                        K.op("dve", lambda e: e.tensor_copy(out=hT[:, kc, :], in_=pb[:, 0:512]),
                             reads=[pb], writes=[hT])

            def M(tb):
                hT = hnT[tb % 2]
                rtb = rt[tb % 2] if rope_tabs else None
                pend = [None]

                def rope_tail():
                    if pend[0] is None:
                        return
                    ps_, xb_, kind_, dstf_ = pend[0]
                    pend[0] = None
                    ps2 = next_ps()
                    K.op("pe", lambda e: e.matmul(ps2[:], lhsT=permT[:], rhs=xb_[:], start=True, stop=True),
                         reads=[permT, xb_], writes=[ps2])
                    ci, si = (0, 1) if kind_ == "ropeq" else (2, 3)
                    a_, b_ = t1[cnt["o16"] % 2], t2[cnt["o16"] % 2]
                    K.op("dve", lambda e: e.tensor_tensor(out=a_[:], in0=ps_[:], in1=rtb[:, ci, :], op=ALU.mult),
                         reads=[ps_, rtb], writes=[a_])
                    K.op("dve", lambda e: e.tensor_tensor(out=b_[:], in0=ps2[:], in1=rtb[:, si, :], op=ALU.mult),
                         reads=[ps2, rtb], writes=[b_])
                    ob = next_o16()
                    K.op("pool", lambda e: e.tensor_tensor(out=ob[:], in0=a_[:], in1=b_[:], op=ALU.add),
                         reads=[a_, b_], writes=[ob])
                    K.dma(dstf_(tb), ob[:], ob, False)

                for (ft, kind, dstf) in fm_specs:
                    ps = next_ps()
                    for kc in range(8):
                        K.op("pe", lambda e: e.matmul(ps[:], lhsT=w[:, kc, ft * 128:(ft + 1) * 128], rhs=hT[:, kc, :],
                                                      start=(kc == 0), stop=(kc == 7)),
                             reads=[w, hT], writes=[ps], sig=(kc == 7))
                    if kind in ("ropeq", "ropek"):
                        xb = xbs[cnt["xb"] % 3]
                        cnt["xb"] += 1
                        K.op("dve", lambda e: e.tensor_copy(out=xb[:], in_=ps[:]), reads=[ps], writes=[xb])
                        rope_tail()
                        pend[0] = (ps, xb, kind, dstf)
                    elif kind in ("f32", "silu"):
                        rope_tail()
                        ob = next_o32()
                        evac(ob, ps, kind)
                        K.dma(dstf(tb), ob[:], ob, False)
                    else:
                        rope_tail()
                        ob = next_o16()
                        evac(ob, ps, kind)
                        K.dma(dstf(tb), ob[:], ob, False)
                rope_tail()
                for (c0, kind, dstf) in tm_specs:
                    for jj in range(4):
                        ps = next_ps()
                        for kc in range(8):
                            K.op("pe", lambda e: e.matmul(ps[:], lhsT=hT[:, kc, jj * 128:(jj + 1) * 128],
                                                          rhs=w[:, kc, c0:c0 + 512], start=(kc == 0), stop=(kc == 7)),
                                 reads=[w, hT], writes=[ps], sig=(kc == 7))
                        if kind == "silu":
                            ob = next_o32()
                        else:
                            ob = next_o16()
                        evac(ob, ps, kind)
                        K.dma(dstf(tb * 512 + jj * 128), ob[:], ob, False)

            N1(0)
            N2(0)
            N1(1)
            for tb in range(8):
                if tb + 1 < 8:
                    N2(tb + 1)
                if tb + 2 < 8:
                    N1(tb + 2)
                M(tb)
            K.barrier()

    def inproj_even(self, li, src):
        j = li // 2

        def rows(t, r0):
            return lambda tb: t[r0:r0 + 128, tb * 512:(tb + 1) * 512]

        fm = []
        for i in range(4):
            fm.append((i, "f32", rows(self.xaT, i * 128)))
        for i in range(4):
            fm.append((4 + i, "silu", rows(self.g1T, i * 128)))
        for i in range(4):
            fm.append((8 + i, "scale8", rows(self.qT, i * 128)))
        for i in range(4):
            fm.append((12 + i, "bf16", rows(self.kT, i * 128)))
        for i in range(4):
            fm.append((20 + i, "silu", rows(self.g1T, 512 + i * 128)))
        tm = [(2048, "bf16", lambda t0: self.v[t0:t0 + 128, 0:512])]
        self.inproj(src, self.norm_mix[li:li + 1, :], self.w_in_e[j], 0, 3072, fm, tm)

    def inproj_odd(self, li, src):
        j = li // 2

        def rows(t, r0):
            return lambda tb: t[r0:r0 + 128, tb * 512:(tb + 1) * 512]

        fm = []
        for i in range(8):
            fm.append((i, "ropeq", rows(self.qT, i * 128)))
        for i in range(8):
            fm.append((8 + i, "ropek", rows(self.kT, i * 128)))
        tm = []
        for c in range(2):
            tm.append((2048 + c * 512, "bf16", (lambda c: lambda t0: self.v[t0:t0 + 128, c * 512:(c + 1) * 512])(c)))
        for c in range(2):
            tm.append((3072 + c * 512, "silu",
                       (lambda c: lambda t0: self.gtok[t0:t0 + 128, c * 512:(c + 1) * 512])(c)))
        self.inproj(src, self.norm_mix[li:li + 1, :], self.w_in_o[j], 0, 4096, fm, tm, rope_tabs=True, perm=True)

    def lru_phase(self, li):
        K = self.K
        j = li // 2
        with ExitStack() as ts:
            pv = self.sb(ts, "pv", [128, 4, 8])
            K.dma(pv[:], self.pvec_e[j], pv, True)
            c1 = self.sb(ts, "c1", [128, 4])
            K.op("act", lambda e: e.activation(out=c1[:], in_=pv[:, :, 7], func=AF.Exp, scale=-1.0),
                 reads=[pv], writes=[c1])
            K.op("act", lambda e: e.activation(out=c1[:], in_=c1[:], func=AF.Ln, bias=1.0), reads=[c1], writes=[c1])
            K.op("dve", lambda e: e.tensor_scalar(out=c1[:], in0=c1[:], scalar1=-8.0, scalar2=None, op0=ALU.mult),
                 reads=[c1], writes=[c1])
            wst = self.sb(ts, "wst", [128, 128])
            wbd = [self.sb(ts, "wbd%d" % i, [128, 4, 128], BF16) for i in range(2)]
            for wi, wsrc in enumerate((self.lru_wa, self.lru_wx)):
                for ct in range(4):
                    K.op("pool", lambda e: e.memset(wst[:], 0.0), writes=[wst])
                    K.dma(wst[0:64, 0:64], wsrc[j, 2 * ct], wst, True)
                    K.dma(wst[64:128, 64:128], wsrc[j, 2 * ct + 1], wst, True)
                    K.op("pool", lambda e: e.tensor_copy(out=wbd[wi][:, ct, :], in_=wst[:]),
                         reads=[wst], writes=[wbd[wi]])
            CH = 1024
            NCH = S // CH
            xp = [self.sb(ts, "xp%d" % i, [128, CH + 3]) for i in range(3)]
            xc = [self.sb(ts, "xc%d" % i, [128, CH]) for i in range(2)]
            xcb = [self.sb(ts, "xcb%d" % i, [128, CH], BF16) for i in range(2)]
            r = [self.sb(ts, "lr%d" % i, [128, CH]) for i in range(3)]
            ii = [self.sb(ts, "li%d" % i, [128, CH]) for i in range(3)]
            a2 = [self.sb(ts, "la2%d" % i, [128, CH]) for i in range(2)]
            hh = [self.sb(ts, "lhh%d" % i, [128, CH]) for i in range(2)]
            ga = [self.sb(ts, "lga%d" % i, [128, CH]) for i in range(3)]
            ob = [self.sb(ts, "lob%d" % i, [128, CH], BF16) for i in range(2)]
            its = [(ct, ch) for ct in range(4) for ch in range(NCH)]
            cnt = [0]

            def LD(it):
                ct, ch = its[it]
                b = it % 3
                c0 = ch * CH
                rows = slice(ct * 128, (ct + 1) * 128)
                if ch == 0:
                    K.op("pool", lambda e: e.memset(xp[b][:, 0:3], 0.0), writes=[xp[b]])
                    K.dma(xp[b][:, 3:CH + 3], self.xaT[rows, 0:CH], xp[b], True)
                else:
                    K.dma(xp[b][:], self.xaT[rows, c0 - 3:c0 + CH], xp[b], True)
                K.dma(ga[b][:], self.g1T[rows, c0:c0 + CH], ga[b], True)

            def P(it):
                ct, ch = its[it]
                b = it % 3
                b2 = it % 2
                c0 = ch * CH
                rows = slice(ct * 128, (ct + 1) * 128)
                K.op("dve", lambda e: e.tensor_scalar(out=xc[b2][:], in0=xp[b][:, 0:CH], scalar1=pv[:, ct, 0:1],
                                                      scalar2=pv[:, ct, 4:5], op0=ALU.mult, op1=ALU.add),
                     reads=[xp[b], pv], writes=[xc[b2]])
                for k in range(1, 4):
                    K.op("dve", lambda e: e.scalar_tensor_tensor(out=xc[b2][:], in0=xp[b][:, k:k + CH],
                                                                 scalar=pv[:, ct, k:k + 1], in1=xc[b2][:],
                                                                 op0=ALU.mult, op1=ALU.add),
                         reads=[xp[b], pv, xc[b2]], writes=[xc[b2]])
                K.op("act", lambda e: e.activation(out=xcb[b2][:], in_=xc[b2][:], func=AF.Copy),
                     reads=[xc[b2]], writes=[xcb[b2]])
                for blk in range(CH // 512):
                    for wi, dst, bcol in ((0, r[b], 5), (1, ii[b], 6)):
                        ps = self.ps[cnt[0] % 4]
                        cnt[0] += 1
                        K.op("pe", lambda e: e.matmul(ps[:], lhsT=wbd[wi][:, ct, :], rhs=xcb[b2][:, blk * 512:(blk + 1) * 512],
                                                      start=True, stop=True), reads=[wbd[wi], xcb[b2]], writes=[ps])
                        K.op("act", lambda e: e.activation(out=dst[:, blk * 512:(blk + 1) * 512], in_=ps[:],
                                                           func=AF.Sigmoid, bias=pv[:, ct, bcol:bcol + 1]),
                             reads=[ps, pv], writes=[dst])

            def P1b(it):
                b = it % 3
                b2 = it % 2
                K.op("dve", lambda e: e.tensor_tensor(out=ii[b][:], in0=ii[b][:], in1=xc[b2][:], op=ALU.mult),
                     reads=[ii[b], xc[b2]], writes=[ii[b]])

            def P2(it):
                ct, ch = its[it]
                b = it % 3
                b2 = it % 2
                K.op("act", lambda e: e.activation(out=r[b][:], in_=r[b][:], func=AF.Exp, scale=c1[:, ct:ct + 1]),
                     reads=[r[b], c1], writes=[r[b]])
                K.op("act", lambda e: e.activation(out=a2[b2][:], in_=r[b][:], func=AF.Square), reads=[r[b]], writes=[a2[b2]])
                K.op("act", lambda e: e.activation(out=a2[b2][:], in_=a2[b2][:], func=AF.Sqrt, scale=-1.0, bias=1.0),
                     reads=[a2[b2]], writes=[a2[b2]])
                K.op("pool", lambda e: e.tensor_tensor(out=ii[b][:], in0=ii[b][:], in1=a2[b2][:], op=ALU.mult),
                     reads=[ii[b], a2[b2]], writes=[ii[b]])

            def Q(it):
                ct, ch = its[it]
                b = it % 3
                b2 = it % 2
                c0 = ch * CH
                rows = slice(ct * 128, (ct + 1) * 128)
                if ch == 0:
                    K.op("dve", lambda e: e.tensor_tensor_scan(out=hh[b2][:], data0=r[b][:], data1=ii[b][:], initial=0.0,
                                                               op0=ALU.mult, op1=ALU.add),
                         reads=[r[b], ii[b]], writes=[hh[b2]])
                else:
                    K.op("dve", lambda e: e.tensor_tensor_scan(out=hh[b2][:], data0=r[b][:], data1=ii[b][:],
                                                               initial=hh[(it - 1) % 2][:, CH - 1:CH],
                                                               op0=ALU.mult, op1=ALU.add),
                         reads=[r[b], ii[b], hh[(it - 1) % 2]], writes=[hh[b2]])
                K.op("dve", lambda e: e.tensor_tensor(out=ob[b2][:], in0=hh[b2][:], in1=ga[b][:], op=ALU.mult),
                     reads=[hh[b2], ga[b]], writes=[ob[b2]])
                K.dma(self.mixT[rows, c0:c0 + CH], ob[b2][:], ob[b2], False)

            LD(0)
            LD(1)
            LD(2)
            P(0)
            P1b(0)
            P(1)
            P1b(1)
            P2(0)
            for it in range(len(its)):
                if it + 2 < len(its):
                    P(it + 2)
                Q(it)
                if it + 3 < len(its):
                    LD(it + 3)
                if it + 2 < len(its):
                    P1b(it + 2)
                if it + 1 < len(its):
                    P2(it + 1)
            K.barrier()

    def sb_phase(self, li):
        K = self.K
        with ExitStack() as ts:
            stg = self.sb(ts, "sb_stg", [128, 4, 512])
            ntinc = self.sb(ts, "ntinc", [128, 128], BF16)
            nones = self.sb(ts, "nones", [128, 128], BF16)
            masks = self.sb(ts, "masks", [128, 4, 512], BF16)
            K.dma(stg[:, 0, 0:128], self.c_ntinc, stg, True)
            K.op("dve", lambda e: e.tensor_copy(out=ntinc[:], in_=stg[:, 0, 0:128]), reads=[stg], writes=[ntinc])
            K.op("pool", lambda e: e.memset(nones[:], -1.0), writes=[nones])
            K.dma(stg[:], self.c_mask_s, stg, True)
            K.op("dve", lambda e: e.tensor_copy(out=masks[:], in_=stg[:]), reads=[stg], writes=[masks])
            kTs = [self.sb(ts, "kT%d" % i, [128, S], BF16) for i in range(2)]
            qTs = [self.sb(ts, "qT%d" % i, [128, S], BF16) for i in range(2)]
            vvs = [self.sb(ts, "vv%d" % i, [128, NT, 128], BF16) for i in range(2)]
            e32 = [self.sb(ts, "e32_%d" % i, [128, 1024]) for i in range(2)]
            spb = [self.sb(ts, "spb%d" % i, [128, 1024], BF16) for i in range(2)]
            ssum = [self.sb(ts, "ssum%d" % i, [128, 512]) for i in range(2)]
            NSB = 6
            ssb = [self.sb(ts, "ssb%d" % i, [128, 512], BF16) for i in range(NSB)]
            wb2 = [self.sb(ts, "wb2_%d" % i, [128, 1024], BF16) for i in range(2)]
            gbt = [self.sb(ts, "gbt%d" % i, [64, 512]) for i in range(2)]
            obs = [self.sb(ts, "obs%d" % i, [64, 512], BF16) for i in range(2)]
            QA = self.pq2
            QO = self.po

            tiles = []
            grp = 0
            for hp in range(4):
                for hh in range(2):
                    for qb in range(8):
                        nk = 4 * qb + 4
                        for i, kt in enumerate(reversed(range(nk))):
                            tiles.append(dict(hp=hp, hh=hh, qb=qb, kt=kt, first=(i == 0), last=(i == nk - 1),
                                              grp=grp, d=kt - 4 * qb, idx=i))
                        grp += 1
            N = len(tiles)
            NP = N // 2
            loaded = set()

            def ensure_loaded(hp):
                if hp in loaded:
                    return
                loaded.add(hp)
                K.dma(kTs[hp % 2][:], self.kT[hp * 128:(hp + 1) * 128, :], kTs[hp % 2], True)
                K.dma(qTs[hp % 2][:], self.qT[hp * 128:(hp + 1) * 128, :], qTs[hp % 2], True)
                K.dma(vvs[hp % 2][:], self.v[:, hp * 128:(hp + 1) * 128].rearrange("(t p) c -> p t c", p=128),
                      vvs[hp % 2], True)

            def s1(p):
                A = QA[p % 3]
                eb = e32[p % 2]
                sp2 = spb[p % 2]
                for h_ in range(2):
                    n = 2 * p + h_
                    t = tiles[n]
                    hp, hh, qb, kt, d = t["hp"], t["hh"], t["qb"], t["kt"], t["d"]
                    ensure_loaded(hp)
                    kT, qT = kTs[hp % 2], qTs[hp % 2]
                    pl, ph = hh * 64, hh * 64 + 64
                    cs = slice(h_ * 512, (h_ + 1) * 512)
                    K.op("pe", lambda e: e.matmul(A[:, cs], lhsT=kT[pl:ph, kt * 128:(kt + 1) * 128],
                                                  rhs=qT[pl:ph, qb * 512:(qb + 1) * 512], start=True, stop=False,
                                                  skip_group_check=True),
                         reads=[kT, qT], writes=[A], sig=(d < 0 and h_ == 1))
                    if d >= 0:
                        K.op("pe", lambda e: e.matmul(A[:, cs], lhsT=self.ident[:], rhs=masks[:, d, :], start=False,
                                                      stop=False, skip_group_check=True),
                             reads=[self.ident, masks], writes=[A], sig=(h_ == 1))
                K.op("act", lambda e: e.activation(out=eb[:], in_=A[:], func=AF.Exp), reads=[A], writes=[eb])
                K.op("act", lambda e: e.activation(out=sp2[:], in_=eb[:], func=AF.Ln, bias=1.0), reads=[eb], writes=[sp2])
                for h_ in range(2):
                    n = 2 * p + h_
                    t = tiles[n]
                    cs = slice(h_ * 512, (h_ + 1) * 512)
                    if not t["last"]:
                        i = t["idx"]
                        sn = ssb[(n + 1) % NSB]
                        if t["first"]:
                            K.op("dve", lambda e: e.tensor_copy(out=ssum[i % 2][:], in_=sp2[:, cs]), reads=[sp2],
                                 writes=[ssum[i % 2]])
                            K.op("pool", lambda e: e.tensor_copy(out=sn[:], in_=sp2[:, cs]), reads=[sp2], writes=[sn])
                        else:
                            K.op("dve", lambda e: e.tensor_tensor(out=ssum[i % 2][:], in0=ssum[(i - 1) % 2][:],
                                                                  in1=sp2[:, cs], op=ALU.add),
                                 reads=[ssum[(i - 1) % 2], sp2], writes=[ssum[i % 2]])
                            K.op("dve", lambda e: e.tensor_copy(out=sn[:], in_=ssum[i % 2][:]), reads=[ssum[i % 2]],
                                 writes=[sn])

            def s2(p):
                A = QA[p % 3]
                sp2 = spb[p % 2]
                for h_ in range(2):
                    n = 2 * p + h_
                    t = tiles[n]
                    cs = slice(h_ * 512, (h_ + 1) * 512)
                    K.op("pe", lambda e: e.matmul(A[:, cs], lhsT=ntinc[:], rhs=sp2[:, cs], start=False, stop=t["first"],
                                                  skip_group_check=True),
                         reads=[ntinc, sp2], writes=[A], sig=(t["first"] and h_ == 1))
                    if not t["first"]:
                        sc = ssb[n % NSB]
                        K.op("pe", lambda e: e.matmul(A[:, cs], lhsT=nones[:], rhs=sc[:], start=False, stop=True,
                                                      skip_group_check=True),
                             reads=[nones, sc], writes=[A], sig=(h_ == 1))
                w2 = wb2[p % 2]
                K.op("act", lambda e: e.activation(out=w2[:], in_=A[:], func=AF.Exp), reads=[A], writes=[w2])

            def s3(n):
                t = tiles[n]
                hp, hh, qb, kt = t["hp"], t["hh"], t["qb"], t["kt"]
                vv = vvs[hp % 2]
                O = QO[t["grp"] % 2]
                w2 = wb2[(n // 2) % 2]
                cs = slice((n % 2) * 512, (n % 2) * 512 + 512)
                K.op("pe", lambda e: e.matmul(O[0:64, :], lhsT=vv[:, kt, hh * 64:hh * 64 + 64], rhs=w2[:, cs],
                                              start=t["first"], stop=t["last"]),
                     reads=[vv, w2], writes=[O], sig=True)
                if t["last"]:
                    g = gbt[t["grp"] % 2]
                    ob = obs[t["grp"] % 2]
                    r0 = 512 + hp * 128 + hh * 64
                    K.dma(g[:], self.g1T[r0:r0 + 64, qb * 512:(qb + 1) * 512], g, True)
                    K.op("dve", lambda e: e.tensor_tensor(out=ob[:], in0=O[0:64, :], in1=g[:], op=ALU.mult),
                         reads=[O, g], writes=[ob])
                    K.dma(self.mixT[r0:r0 + 64, qb * 512:(qb + 1) * 512], ob[:], ob, False)

            for p in range(NP + 2):
                if p < NP:
                    s1(p)
                if 0 <= p - 1 < NP:
                    s2(p - 1)
                if 0 <= p - 2 < NP:
                    s3(2 * (p - 2))
                    s3(2 * (p - 2) + 1)
            K.barrier()

    def diff_phase(self, li):
        K = self.K
        j = li // 2
        lam_init = 0.8 - 0.6 * math.exp(-0.3 * li)
        with ExitStack() as ts:
            stg = self.sb(ts, "df_stg", [128, 2, 256])
            masks = self.sb(ts, "dmasks", [128, 2, 256], BF16)
            K.dma(stg[:], self.c_mask_i, stg, True)
            K.op("dve", lambda e: e.tensor_copy(out=masks[:], in_=stg[:]), reads=[stg], writes=[masks])
            lv = self.sb(ts, "lv", [128, 256])
            lt = self.sb(ts, "lt", [128, 64])
            ls = self.sb(ts, "ls", [128, 2])
            neglam = self.sb(ts, "neglam", [128, 1])
            K.dma(lv[:], self.lamv[j].broadcast_to([128, 256]), lv, True)
            for i in range(2):
                K.op("dve", lambda e: e.scalar_tensor_tensor(out=lt[:], in0=lv[:, i * 128:i * 128 + 64], scalar=1.0,
                                                             in1=lv[:, i * 128 + 64:i * 128 + 128], op0=ALU.mult,
                                                             op1=ALU.mult, accum_out=ls[:, i:i + 1]),
                     reads=[lv], writes=[lt, ls])
            K.op("act", lambda e: e.activation(out=ls[:], in_=ls[:], func=AF.Exp), reads=[ls], writes=[ls])
            K.op("dve", lambda e: e.tensor_tensor(out=neglam[:], in0=ls[:, 1:2], in1=ls[:, 0:1], op=ALU.subtract),
                 reads=[ls], writes=[neglam])
            K.op("dve", lambda e: e.tensor_scalar(out=neglam[:], in0=neglam[:], scalar1=-lam_init, scalar2=None,
                                                  op0=ALU.add), reads=[neglam], writes=[neglam])

            kz = [[self.sb(ts, "dkz%d_%d" % (m, i), [128, S], BF16) for i in range(2)] for m in range(2)]
            for m in range(2):
                for i in range(2):
                    K.op("pool", lambda e: e.memset(kz[m][i][:], 0.0), writes=[kz[m][i]])
            qTs = [self.sb(ts, "dqT%d" % i, [128, S], BF16) for i in range(2)]
            vas = [self.sb(ts, "va%d" % i, [128, NT, 130], BF16) for i in range(2)]
            for i in range(2):
                K.op("pool", lambda e: e.memset(vas[i][:, :, 128:130], 1.0), writes=[vas[i]])
            Eb = [self.sb(ts, "Eb%d" % i, [128, 512], BF16) for i in range(3)]
            gts = [self.sb(ts, "gts%d" % i, [128, 2, 128]) for i in range(2)]
            rec = [self.sb(ts, "rec%d" % i, [128, 4]) for i in range(2)]
            tt = [self.sb(ts, "tt%d" % i, [128, 128]) for i in range(2)]
            oo = [self.sb(ts, "oo%d" % i, [128, 128]) for i in range(2)]
            jk = self.sb(ts, "jk", [128, 128])
            ssq = [self.sb(ts, "ssq%d" % i, [128, 1]) for i in range(2)]
            rsd = [self.sb(ts, "rsd%d" % i, [128, 1]) for i in range(2)]
            onb = [self.sb(ts, "onb%d" % i, [128, 128], BF16) for i in range(2)]
            oT = [self.sb(ts, "oT%d" % i, [128, 256], BF16) for i in range(2)]

            tiles = []
            grp = 0
            for h in range(8):
                for qb in range(16):
                    nk = 2 * qb + 2
                    for kt in range(nk):
                        tiles.append(dict(h=h, qb=qb, kt=kt, first=(kt == 0), last=(kt == nk - 1), grp=grp,
                                          d=kt - 2 * qb))
                    grp += 1
            N = len(tiles)
            loaded = set()

            def ensure_loaded(h):
                if h in loaded:
                    return
                loaded.add(h)
                for m in range(2):
                    K.dma(kz[m][h % 2][m * 64:m * 64 + 64, :], self.kT[h * 128 + m * 64:h * 128 + m * 64 + 64, :],
                          kz[m][h % 2], True)
                K.dma(qTs[h % 2][:], self.qT[h * 128:(h + 1) * 128, :], qTs[h % 2], True)
                K.dma(vas[h % 2][:, :, 0:128], self.v[:, h * 128:(h + 1) * 128].rearrange("(t p) c -> p t c", p=128),
                      vas[h % 2], True)

            E2 = [self.sb(ts, "E2_%d" % i, [128, 1024], BF16) for i in range(3)]
            QA = self.pq2
            XY = [(self.ps[4], self.ps[5]), (self.po[0], self.po[1])]

            def s1(p):
                A = QA[p % 2]
                for h_ in range(2):
                    n = 2 * p + h_
                    t = tiles[n]
                    h, qb, kt, d = t["h"], t["qb"], t["kt"], t["d"]
                    ensure_loaded(h)
                    qT = qTs[h % 2]
                    for m in range(2):
                        kT = kz[m][h % 2]
                        c0 = h_ * 512 + m * 256
                        K.op("pe", lambda e: e.matmul(A[:, c0:c0 + 256], lhsT=kT[:, kt * 128:(kt + 1) * 128],
                                                      rhs=qT[:, qb * 256:(qb + 1) * 256], start=(m == 0), stop=(d < 0),
                                                      skip_group_check=True),
                             reads=[kT, qT], writes=[A], sig=(d < 0 and m == 1 and h_ == 1))
                    if d >= 0:
                        for m in range(2):
                            c0 = h_ * 512 + m * 256
                            K.op("pe", lambda e: e.matmul(A[:, c0:c0 + 256], lhsT=self.ident[:], rhs=masks[:, d, :],
                                                          start=False, stop=True, skip_group_check=True),
                                 reads=[self.ident, masks], writes=[A], sig=(m == 1 and h_ == 1))
                E = E2[p % 3]
                K.op("act", lambda e: e.activation(out=E[:], in_=A[:], func=AF.Exp), reads=[A], writes=[E])

            def s2(n):
                t = tiles[n]
                h, qb, kt, d = t["h"], t["qb"], t["kt"], t["d"]
                va = vas[h % 2]
                E = E2[(n // 2) % 3]
                e0 = (n % 2) * 512
                X, Y = XY[t["grp"] % 2]
                subs = [s_ for s_ in range(2) if not (d >= 0 and s_ < d)]
                for m, acc in ((0, X), (1, Y)):
                    for s_ in subs:
                        K.op("pe", lambda e: e.matmul(acc[:, s_ * 256:s_ * 256 + 130],
                                                      lhsT=E[:, e0 + m * 256 + s_ * 128:e0 + m * 256 + (s_ + 1) * 128],
                                                      rhs=va[:, kt, :], start=(t["first"] and s_ == 0),
                                                      stop=t["last"], skip_group_check=True),
                             reads=[E, va], writes=[acc], sig=(s_ == subs[-1]))

            NB4 = 4
            e_gt = [self.sb(ts, "e_gt%d" % i, [128, 2, 128]) for i in range(NB4)]
            e_rc = [self.sb(ts, "e_rc%d" % i, [128, 2, 4]) for i in range(NB4)]
            e_tt = [self.sb(ts, "e_tt%d" % i, [128, 2, 128]) for i in range(NB4)]
            e_oo = [self.sb(ts, "e_oo%d" % i, [128, 2, 128]) for i in range(NB4)]
            e_sq = [self.sb(ts, "e_sq%d" % i, [128, 2]) for i in range(NB4)]
            e_rs = [self.sb(ts, "e_rs%d" % i, [128, 2]) for i in range(NB4)]
            e_on = [self.sb(ts, "e_on%d" % i, [128, 2, 128], BF16) for i in range(NB4)]
            e_oT = [self.sb(ts, "e_oT%d" % i, [128, 256], BF16) for i in range(NB4)]

            def epi1(n):
                t = tiles[n]
                h, qb, g = t["h"], t["qb"], t["grp"]
                X, Y = XY[g % 2]
                k4 = g % NB4
                gt, rc, tb_, ob_, sq, rs = e_gt[k4], e_rc[k4], e_tt[k4], e_oo[k4], e_sq[k4], e_rs[k4]
                K.dma(gt[:], self.gtok[qb * 256:(qb + 1) * 256, h * 128:(h + 1) * 128].rearrange("(s p) c -> p s c", p=128),
                      gt, True)
                for s_ in range(2):
                    K.op("dve", lambda e: e.reciprocal(out=rc[:, s_, 0:1], in_=X[:, s_ * 256 + 128:s_ * 256 + 129]),
                         reads=[X], writes=[rc])
                    K.op("dve", lambda e: e.reciprocal(out=rc[:, s_, 1:2], in_=Y[:, s_ * 256 + 128:s_ * 256 + 129]),
                         reads=[Y], writes=[rc])
                    K.op("dve", lambda e: e.tensor_scalar(out=rc[:, s_, 2:3], in0=rc[:, s_, 1:2], scalar1=neglam[:, 0:1],
                                                          scalar2=None, op0=ALU.mult), reads=[rc, neglam], writes=[rc])
                    K.op("dve", lambda e: e.tensor_scalar(out=tb_[:, s_, :], in0=Y[:, s_ * 256:s_ * 256 + 128],
                                                          scalar1=rc[:, s_, 2:3], scalar2=None, op0=ALU.mult),
                         reads=[Y, rc], writes=[tb_])
                    K.op("dve", lambda e: e.scalar_tensor_tensor(out=ob_[:, s_, :], in0=X[:, s_ * 256:s_ * 256 + 128],
                                                                 scalar=rc[:, s_, 0:1], in1=tb_[:, s_, :], op0=ALU.mult,
                                                                 op1=ALU.add), reads=[X, rc, tb_], writes=[ob_])
                    K.op("dve", lambda e: e.scalar_tensor_tensor(out=tb_[:, s_, :], in0=ob_[:, s_, :], scalar=1.0,
                                                                 in1=ob_[:, s_, :], op0=ALU.mult, op1=ALU.mult,
                                                                 accum_out=sq[:, s_:s_ + 1]),
                         reads=[ob_], writes=[tb_, sq])
                K.op("dve", lambda e: e.tensor_scalar(out=rs[:], in0=sq[:], scalar1=1.0 / 128, scalar2=EPS,
                                                      op0=ALU.mult, op1=ALU.add), reads=[sq], writes=[rs])
                K.op("pool", lambda e: e.tensor_tensor(out=ob_[:], in0=ob_[:], in1=gt[:], op=ALU.mult),
                     reads=[ob_, gt], writes=[ob_])

            def epi2(n):
                k4 = tiles[n]["grp"] % NB4
                rs = e_rs[k4]
                K.op("act", lambda e: e.activation(out=rs[:], in_=rs[:], func=AF.Ln), reads=[rs], writes=[rs])
                K.op("act", lambda e: e.activation(out=rs[:], in_=rs[:], func=AF.Exp, scale=-0.5), reads=[rs], writes=[rs])

            def epi3(n):
                t = tiles[n]
                h, qb = t["h"], t["qb"]
                k4 = t["grp"] % NB4
                ob_, rs, on = e_oo[k4], e_rs[k4], e_on[k4]
                for s_ in range(2):
                    K.op("dve", lambda e: e.tensor_scalar(out=on[:, s_, :], in0=ob_[:, s_, :], scalar1=rs[:, s_:s_ + 1],
                                                          scalar2=None, op0=ALU.mult), reads=[ob_, rs], writes=[on])
                K.dma(self.mixtok[qb * 256:(qb + 1) * 256, h * 128:(h + 1) * 128].rearrange("(s p) c -> p s c", p=128),
                      on[:], on, False)

            NP = N // 2
            pending = []
            for p in range(NP + 1):
                if p < NP:
                    s1(p)
                if 0 <= p - 1 < NP:
                    for n in (2 * (p - 1), 2 * (p - 1) + 1):
                        s2(n)
                        if tiles[n]["last"]:
                            pending.append((p, epi1, n))
                            pending.append((p + 1, epi2, n))
                            pending.append((p + 2, epi3, n))
                            pending.sort(key=lambda x: x[0])
                while pending and pending[0][0] <= p:
                    _, fn, arg = pending.pop(0)
                    fn(arg)
            while pending:
                _, fn, arg = pending.pop(0)
                fn(arg)
            K.barrier()

    def out_phase(self, li, src):
        K = self.K
        j = li // 2
        odd = (li % 2 == 1)
        lam_init = 0.8 - 0.6 * math.exp(-0.3 * li)
        with ExitStack() as ts:
            stgs = [self.sb(ts, "o_stg%d" % i, [128, 1024]) for i in range(3)]
            n = [0]

            def lw(name, wsrc, kcn, scale=None):
                w = self.sb(ts, name, [128, kcn, D], BF16)
                for kc in range(kcn):
                    stg = stgs[n[0] % 3]
                    eng = "act" if n[0] % 2 == 0 else "dve"
                    n[0] += 1
                    K.dma(stg[:], wsrc[kc * 128:(kc + 1) * 128, :], stg, True)
                    if scale is None and eng == "act":
                        K.op("act", lambda e: e.activation(out=w[:, kc, :], in_=stg[:], func=AF.Copy),
                             reads=[stg], writes=[w])
                    elif scale is None:
                        K.op(eng, lambda e: e.tensor_copy(out=w[:, kc, :], in_=stg[:]), reads=[stg], writes=[w])
                    else:
                        K.op("dve", lambda e: e.tensor_scalar(out=w[:, kc, :], in0=stg[:], scalar1=scale[:, 0:1],
                                                              scalar2=None, op0=ALU.mult),
                             reads=[stg, scale], writes=[w])
                return w

            scale = None
            if odd:
                sg = self.sb(ts, "sg", [128, 2])
                scale = self.sb(ts, "sgs", [128, 1])
                K.dma(sg[:], self.subg, sg, True)
                K.op("dve", lambda e: e.tensor_scalar(out=scale[:], in0=sg[:, j:j + 1], scalar1=(1.0 - lam_init),
                                                      scalar2=None, op0=ALU.mult), reads=[sg], writes=[scale])
            wo = lw("wo", (self.w_out_o if odd else self.w_out_e)[j], 8, scale)
            wg = lw("wg", self.w_ple_gate[li], 8)
            wp = lw("wp", self.w_ple_proj[li], 2)
            gbc = self.sb(ts, "gple", [128, D])
            K.dma(gbc[:], self.norm_ple[li:li + 1, :].broadcast_to([128, D]), gbc, True)
            mT = [self.sb(ts, "mT%d" % i, [128, 8, 128], BF16) for i in range(3)]
            hx = [self.sb(ts, "ohx%d" % i, [128, D]) for i in range(3)]
            pt = [self.sb(ts, "pt%d" % i, [128, 256]) for i in range(3)]
            ptb = [self.sb(ts, "ptb%d" % i, [128, 256], BF16) for i in range(2)]
            h1 = [self.sb(ts, "h1_%d" % i, [128, D]) for i in range(2)]
            junk = self.sb(ts, "ojunk", [128, D])
            ss = [self.sb(ts, "oss%d" % i, [128, 1]) for i in range(2)]
            rs = [self.sb(ts, "ors%d" % i, [128, 1]) for i in range(2)]
            hn2 = [self.sb(ts, "hn2_%d" % i, [128, D], BF16) for i in range(2)]
            hn2T = [self.sb(ts, "hn2T%d" % i, [128, 8, 128], BF16) for i in range(2)]
            pT = [self.sb(ts, "pT%d" % i, [128, 2, 128], BF16) for i in range(2)]
            gs = [self.sb(ts, "gs%d" % i, [128, D]) for i in range(2)]
            h2 = [self.sb(ts, "h2_%d" % i, [128, D]) for i in range(2)]
            mtok = [self.sb(ts, "mtok%d" % i, [128, D], BF16) for i in range(4)] if odd else None

            def LD(tt):
                b3 = tt % 3
                r0 = tt * 128
                if odd:
                    K.dma(mtok[tt % 4][:], self.mixtok[r0:r0 + 128, :], mtok[tt % 4], True)
                else:
                    K.dma(mT[b3][:], self.mixT[:, r0:r0 + 128].rearrange("(kc p) t -> p kc t", p=128), mT[b3], True)
                K.dma(hx[b3][:], src[r0:r0 + 128, :], hx[b3], True)
                K.dma(pt[b3][:], self.p[li, r0:r0 + 128, :], pt[b3], True)

            def TR(tt):
                b3 = tt % 3
                mk = mtok[tt % 4]
                pbm = self.pb[1]
                for kc in range(8):
                    K.op("pe", lambda e: e.transpose(out=pbm[:, kc * 128:(kc + 1) * 128],
                                                     in_=mk[:, kc * 128:(kc + 1) * 128], identity=self.ident[:]),
                         reads=[mk, self.ident], writes=[pbm], sig=(kc == 7))
                K.op("dve", lambda e: e.tensor_copy(out=mT[b3][:].rearrange("p k t -> p (k t)"), in_=pbm[:]),
                     reads=[pbm], writes=[mT[b3]])

            def A1(tt):
                b = tt % 2
                b3 = tt % 3
                for c in range(2):
                    ps = self.ps[c]
                    for kc in range(8):
                        K.op("pe", lambda e: e.matmul(ps[:], lhsT=mT[b3][:, kc, :], rhs=wo[:, kc, c * 512:(c + 1) * 512],
                                                      start=(kc == 0), stop=(kc == 7)),
                             reads=[mT[b3], wo], writes=[ps], sig=(kc == 7))
                    K.op("dve", lambda e: e.tensor_tensor(out=h1[b][:, c * 512:(c + 1) * 512], in0=ps[:],
                                                          in1=hx[b3][:, c * 512:(c + 1) * 512], op=ALU.add),
                         reads=[ps, hx[b3]], writes=[h1[b]])
                K.op("act", lambda e: e.activation(out=junk[:], in_=h1[b][:], func=AF.Square, accum_out=ss[b][:, 0:1]),
                     reads=[h1[b]], writes=[junk, ss[b]])
                self.rms_scale(ss[b], rs[b], 1, D)
                K.op("dve", lambda e: e.scalar_tensor_tensor(out=hn2[b][:], in0=h1[b][:], scalar=rs[b][:, 0:1],
                                                             in1=gbc[:], op0=ALU.mult, op1=ALU.mult),
                     reads=[h1[b], rs[b], gbc], writes=[hn2[b]])
                K.op("pool", lambda e: e.tensor_copy(out=ptb[b][:], in_=pt[b3][:]), reads=[pt[b3]], writes=[ptb[b]])

            def A2(tt):
                b = tt % 2
                pbk = self.pb[0]
                for kc in range(8):
                    K.op("pe", lambda e: e.transpose(out=pbk[:, kc * 128:(kc + 1) * 128], in_=hn2[b][:, kc * 128:(kc + 1) * 128],
                                                     identity=self.ident[:]),
                         reads=[hn2[b], self.ident], writes=[pbk], sig=(kc == 7))
                K.op("act", lambda e: e.activation(out=hn2T[b][:].rearrange("p k t -> p (k t)"), in_=pbk[:], func=AF.Copy),
                     reads=[pbk], writes=[hn2T[b]])
                pbp = self.pb[1]
                for kc in range(2):
                    K.op("pe", lambda e: e.transpose(out=pbp[:, kc * 128:(kc + 1) * 128], in_=ptb[b][:, kc * 128:(kc + 1) * 128],
                                                     identity=self.ident[:]),
                         reads=[ptb[b], self.ident], writes=[pbp], sig=(kc == 1))
                K.op("act", lambda e: e.activation(out=pT[b][:].rearrange("p k t -> p (k t)"), in_=pbp[:, 0:256],
                                                   func=AF.Copy),
                     reads=[pbp], writes=[pT[b]])

            def B(tt):
                b = tt % 2
                r0 = tt * 128
                for c in range(2):
                    pg = self.ps[2 + c]
                    pp = self.ps[4 + c]
                    for kc in range(8):
                        K.op("pe", lambda e: e.matmul(pg[:], lhsT=hn2T[b][:, kc, :], rhs=wg[:, kc, c * 512:(c + 1) * 512],
                                                      start=(kc == 0), stop=(kc == 7)),
                             reads=[hn2T[b], wg], writes=[pg], sig=(kc == 7))
                    for kc in range(2):
                        K.op("pe", lambda e: e.matmul(pp[:], lhsT=pT[b][:, kc, :], rhs=wp[:, kc, c * 512:(c + 1) * 512],
                                                      start=(kc == 0), stop=(kc == 1)),
                             reads=[pT[b], wp], writes=[pp], sig=(kc == 1))
                    cs = slice(c * 512, (c + 1) * 512)
                    K.op("act", lambda e: e.activation(out=gs[b][:, cs], in_=pg[:], func=AF.Sigmoid),
                         reads=[pg], writes=[gs[b]])
                    K.op("dve", lambda e: e.tensor_tensor(out=gs[b][:, cs], in0=gs[b][:, cs], in1=pp[:], op=ALU.mult),
                         reads=[gs[b], pp], writes=[gs[b]])
                    K.op("pool", lambda e: e.tensor_tensor(out=h2[b][:, cs], in0=gs[b][:, cs], in1=h1[b][:, cs], op=ALU.add),
                         reads=[gs[b], h1[b]], writes=[h2[b]])
                K.dma(self.h[r0:r0 + 128, :], h2[b][:], h2[b], False)

            LD(0)
            LD(1)
            LD(2)
            if odd:
                TR(0)
                TR(1)
            A1(0)
            A2(0)
            for tt in range(NT):
                if tt + 3 < NT:
                    LD(tt + 3)
                if odd and tt + 2 < NT:
                    TR(tt + 2)
                if tt + 1 < NT:
                    A1(tt + 1)
                B(tt)
                if tt + 1 < NT:
                    A2(tt + 1)
            K.barrier()

    def final_phase(self, src):
        K = self.K
        with ExitStack() as ts:
            gbc = self.sb(ts, "gfin", [128, D])
            K.dma(gbc[:], self.final_norm.broadcast_to([128, D]), gbc, True)
            hx = [self.sb(ts, "fhx%d" % i, [128, 4, D]) for i in range(2)]
            ho = [self.sb(ts, "fho%d" % i, [128, 4, D]) for i in range(2)]
            junk = self.sb(ts, "fjunk", [128, D])
            ss = [self.sb(ts, "fss%d" % i, [128, 4]) for i in range(2)]
            rs = [self.sb(ts, "frs%d" % i, [128, 4]) for i in range(2)]
            for tb in range(8):
                b = tb % 2
                K.dma(hx[b][:], src[tb * 512:(tb + 1) * 512, :].rearrange("(j p) d -> p j d", p=128), hx[b], True)
                for jj in range(4):
                    K.op("act", lambda e: e.activation(out=junk[:], in_=hx[b][:, jj, :], func=AF.Square,
                                                       accum_out=ss[b][:, jj:jj + 1]),
                         reads=[hx[b]], writes=[junk, ss[b]])
                self.rms_scale(ss[b], rs[b], 4, D)
                for jj in range(4):
                    K.op("dve", lambda e: e.scalar_tensor_tensor(out=ho[b][:, jj, :], in0=hx[b][:, jj, :],
                                                               scalar=rs[b][:, jj:jj + 1], in1=gbc[:],
                                                               op0=ALU.mult, op1=ALU.mult),
                         reads=[hx[b], rs[b], gbc], writes=[ho[b]])
                K.dma(self.out[tb * 512:(tb + 1) * 512, :].rearrange("(j p) d -> p j d", p=128), ho[b][:], ho[b], False)
            K.barrier()


def _consts():
    ident = np.eye(128, dtype=np.float32)
    jj = np.arange(128)[:, None]
    ss_ = np.arange(128)[None, :]
    ntinc = np.where(jj >= ss_, -1.0, 0.0).astype(np.float32)
    k = np.arange(128)[:, None, None]
    d = np.arange(4)[None, :, None]
    q = np.arange(512)[None, None, :]
    mask_s = np.where(d * 128 + k < q, 0.0, NEG).astype(np.float32)
    d2 = np.arange(2)[None, :, None]
    q2 = np.arange(256)[None, None, :]
    mask_i = np.where(d2 * 128 + k <= q2, 0.0, NEG).astype(np.float32)
    pidx = np.arange(128)
    freq = np.zeros((128, 2), np.float32)
    freq[:, 0] = (10000.0 ** (-(2.0 * (pidx % 32)) / 64.0)).astype(np.float32)
    freq[:, 1] = np.where((pidx % 64) < 32, -1.0, 1.0)
    perm = np.zeros((128, 128), np.float32)
    for i in range(128):
        perm[(i + 32) % 64 + 64 * (i // 64), i] = 1.0
    return dict(c_ident=ident, c_ntinc=ntinc, c_mask_s=mask_s, c_mask_i=mask_i, c_freq=freq, c_perm=perm)


def _shared_inputs(inp):
    f = lambda a: np.ascontiguousarray(np.asarray(a, dtype=np.float32))
    sh = {}
    for k in ("norm_mix", "norm_ple", "w_ple_gate", "w_ple_proj", "w_in_e", "lru_wa", "lru_wx", "w_out_e",
              "w_in_o", "w_out_o"):
        sh[k] = f(inp[k])
    sh["final_norm"] = f(inp["final_norm"]).reshape(1, D)
    pv = np.zeros((2, 128, 4, 8), np.float32)
    for j in range(2):
        for kk in range(4):
            pv[j, :, :, kk] = f(inp["conv_w"])[j, kk].reshape(4, 128).T
        pv[j, :, :, 4] = f(inp["conv_b"])[j].reshape(4, 128).T
        pv[j, :, :, 5] = f(inp["lru_ba"])[j].reshape(4, 128).T
        pv[j, :, :, 6] = f(inp["lru_bx"])[j].reshape(4, 128).T
        pv[j, :, :, 7] = f(inp["lru_lambda"])[j].reshape(4, 128).T
    sh["pvec_e"] = pv
    sh["lamv"] = np.ascontiguousarray(np.concatenate(
        [f(inp["lam_q1"]), f(inp["lam_k1"]), f(inp["lam_q2"]), f(inp["lam_k2"])], axis=1).reshape(2, 1, 256))
    sh["subg"] = np.ascontiguousarray(f(inp["subln_g"]).T)
    sh.update(_consts())
    return sh


_PROG_CACHE = {}


def _get_prog(n_layers=DEPTH, dbg=False):
    key = (n_layers, dbg)
    if key not in _PROG_CACHE:
        _PROG_CACHE[key] = Prog(n_layers, dbg).build()
    return _PROG_CACHE[key]


def kernel(x, p, positions, norm_mix, norm_ple, w_ple_gate, w_ple_proj, w_in_e, conv_w, conv_b, lru_wa, lru_ba,
           lru_wx, lru_bx, lru_lambda, w_out_e, w_in_o, lam_q1, lam_k1, lam_q2, lam_k2, subln_g, w_out_o,
           final_norm):
    inp = dict(norm_mix=norm_mix, norm_ple=norm_ple, w_ple_gate=w_ple_gate, w_ple_proj=w_ple_proj, w_in_e=w_in_e,
               conv_w=conv_w, conv_b=conv_b, lru_wa=lru_wa, lru_ba=lru_ba, lru_wx=lru_wx, lru_bx=lru_bx,
               lru_lambda=lru_lambda, w_out_e=w_out_e, w_in_o=w_in_o, lam_q1=lam_q1, lam_k1=lam_k1, lam_q2=lam_q2,
               lam_k2=lam_k2, subln_g=subln_g, w_out_o=w_out_o, final_norm=final_norm)
    sh = _shared_inputs(inp)
    x = np.asarray(x, dtype=np.float32)
    p = np.asarray(p, dtype=np.float32)
    positions = np.asarray(positions).astype(np.int32)
    nb = x.shape[0]
    nc = _get_prog()
    in_maps = []
    for c in range(nb):
        m = dict(sh)
        m["x"] = np.ascontiguousarray(x[c])
        m["p"] = np.ascontiguousarray(p[:, c])
        m["pos"] = np.ascontiguousarray(positions[c].reshape(1, S))
        in_maps.append(m)
    res = run_bass_kernel_spmd(nc, in_maps, core_ids=list(range(nb)))
    return np.stack([np.asarray(r["out"], dtype=np.float32) for r in res.results], axis=0)
```

```python
import math
from contextlib import ExitStack

import numpy as np
import concourse.bass as bass
import concourse.mybir as mybir
from concourse.bass_utils import run_bass_kernel_spmd

F32 = mybir.dt.float32
BF16 = mybir.dt.bfloat16
I32 = mybir.dt.int32
AF = mybir.ActivationFunctionType
ALU = mybir.AluOpType

S = 4096
D = 1024
NT = S // 128
DEPTH = 4
EPS = 1e-6
NEG = -30000.0
TWO_PI = 2.0 * math.pi
CW1 = 6.28125
CW2 = TWO_PI - CW1


class Sem:
    def __init__(self, handle, key):
        self.h = handle
        self.key = key
        self.total = 0


class Buf:
    def __init__(self, t, name):
        self.t = t
        self.name = name
        self.last_w = None
        self.reads = {}
        self.sem = None

    def __getitem__(self, idx):
        return self.t[idx]


class Kern:
    ENGS = ("pe", "act", "dve", "pool", "sp")

    def __init__(self, nc, es, n_dma_sems=40):
        self.nc = nc
        self.eng = {"pe": nc.tensor, "act": nc.scalar, "dve": nc.vector,
                    "pool": nc.gpsimd, "sp": nc.sync}
        self.csem = {e: Sem(es.enter_context(nc.semaphore("c_" + e)), "c_" + e) for e in self.ENGS}
        self.known = {e: {} for e in self.ENGS}
        self.free_sems = [Sem(es.enter_context(nc.semaphore("d%d" % i)), "d%d" % i)
                          for i in range(n_dma_sems)]
        self.live_sems = []
        self.n_instr = 0

    def _collect(self, e, reads, writes):
        waits = {}

        def need(ev, raw):
            if ev is None:
                return
            s, v, src = ev
            if src == e and e != "sp" and not raw:
                return
            if src == e and e == "pe":
                return
            if self.known[e].get(s.key, 0) >= v:
                return
            if s.key not in waits or waits[s.key][1] < v:
                waits[s.key] = (s, v)

        for b in reads:
            need(b.last_w, True)
        for b in writes:
            need(b.last_w, False)
            for ev in b.reads.values():
                need(ev, False)
        return list(waits.values())

    def _emit(self, e, fn, waits):
        eng = self.eng[e]
        for s, v in waits[:-1]:
            eng.wait_ge(s.h, v)
            self.known[e][s.key] = v
        ins = fn(eng)
        if waits:
            s, v = waits[-1]
            ins._wait_ge(s.h, v)
            self.known[e][s.key] = v
        self.n_instr += 1
        return ins

    def op(self, e, fn, reads=(), writes=(), sig=True):
        waits = self._collect(e, reads, writes)
        ins = self._emit(e, fn, waits)
        cs = self.csem[e]
        if sig:
            cs.total += 1
            ins.then_inc(cs.h, 1)
            ev = (cs, cs.total, e)
        else:
            ev = (cs, cs.total + 1, e)
        for b in writes:
            b.last_w = ev
            b.reads = {}
        for b in reads:
            if b not in writes:
                b.reads[cs.key] = ev
        return ins

    def dma(self, out, in_, buf, load, q="sp"):
        if buf.sem is None:
            buf.sem = self.free_sems.pop()
            self.live_sems.append(buf.sem)
        s = buf.sem
        waits = {}
        evs = []
        if load:
            if buf.last_w is not None and buf.last_w[0] is not s:
                evs.append(buf.last_w)
            evs.extend(ev for ev in buf.reads.values())
        else:
            if buf.last_w is not None:
                evs.append(buf.last_w)
        for ev in evs:
            ss, v, _ = ev
            if self.known[q].get(ss.key, 0) >= v:
                continue
            if ss.key not in waits or waits[ss.key][1] < v:
                waits[ss.key] = (ss, v)
        ins = self._emit(q, lambda eng: eng.dma_start(out=out, in_=in_), list(waits.values()))
        s.total += 16
        ins.then_inc(s.h, 16)
        ev = (s, s.total, None)
        if load:
            buf.last_w = ev
            buf.reads = {}
        else:
            buf.reads[s.key] = ev
        return ins

    def barrier(self):
        sp = self.eng["sp"]
        for s in self.live_sems:
            if self.known["sp"].get(s.key, 0) < s.total:
                sp.wait_ge(s.h, s.total)
                self.known["sp"][s.key] = s.total
        for e in self.ENGS:
            if e == "sp":
                continue
            cs = self.csem[e]
            if self.known["sp"].get(cs.key, 0) < cs.total:
                sp.wait_ge(cs.h, cs.total)
                self.known["sp"][cs.key] = cs.total
        cs = self.csem["sp"]
        cs.total += 1
        sp.sem_inc(cs.h, 1)
        for e in self.ENGS:
            if e == "sp":
                continue
            self.eng[e].wait_ge(cs.h, cs.total)
            self.known[e][cs.key] = cs.total
            for s in self.live_sems:
                self.known[e][s.key] = s.total
            for e2 in self.ENGS:
                self.known[e][self.csem[e2].key] = self.csem[e2].total
        for e2 in self.ENGS:
            self.known["sp"][self.csem[e2].key] = self.csem[e2].total
        self.free_sems.extend(self.live_sems)
        self.live_sems = []


class Prog:
    def __init__(self, n_layers=DEPTH, dbg=False):
        self.n_layers = n_layers
        self.dbg = dbg
        self.nc = bass.Bass("TRN2", target_bir_lowering=False)

    def din(self, name, shape, dt=F32):
        return self.nc.dram_tensor(name, list(shape), dt, kind="ExternalInput").ap()

    def dscr(self, name, shape, dt=F32):
        kind = "ExternalOutput" if self.dbg else "Internal"
        return self.nc.dram_tensor(name, list(shape), dt, kind=kind).ap()

    def sb(self, es, name, shape, dt=F32):
        self._uid = getattr(self, "_uid", 0) + 1
        name = "%s_%d" % (name, self._uid)
        t = es.enter_context(self.nc.sbuf_tensor(name, list(shape), dt))
        return Buf(t, name)

    def build(self):
        nc = self.nc
        self.x = self.din("x", [S, D])
        self.p = self.din("p", [DEPTH, S, 256])
        self.pos = self.din("pos", [1, S], I32)
        self.norm_mix = self.din("norm_mix", [DEPTH, D])
        self.norm_ple = self.din("norm_ple", [DEPTH, D])
        self.w_ple_gate = self.din("w_ple_gate", [DEPTH, D, D])
        self.w_ple_proj = self.din("w_ple_proj", [DEPTH, 256, D])
        self.w_in_e = self.din("w_in_e", [2, D, 3072])
        self.pvec_e = self.din("pvec_e", [2, 128, 4, 8])
        self.lru_wa = self.din("lru_wa", [2, 8, 64, 64])
        self.lru_wx = self.din("lru_wx", [2, 8, 64, 64])
        self.w_out_e = self.din("w_out_e", [2, D, D])
        self.w_in_o = self.din("w_in_o", [2, D, 4096])
        self.lamv = self.din("lamv", [2, 1, 256])
        self.subg = self.din("subg", [128, 2])
        self.w_out_o = self.din("w_out_o", [2, D, D])
        self.final_norm = self.din("final_norm", [1, D])
        self.c_ident = self.din("c_ident", [128, 128])
        self.c_ntinc = self.din("c_ntinc", [128, 128])
        self.c_mask_s = self.din("c_mask_s", [128, 4, 512])
        self.c_mask_i = self.din("c_mask_i", [128, 2, 256])
        self.c_freq = self.din("c_freq", [128, 2])
        self.c_perm = self.din("c_perm", [128, 128])
        self.out = nc.dram_tensor("out", [S, D], F32, kind="ExternalOutput").ap()

        self.h = self.dscr("h_scr", [S, D])
        self.qT = self.dscr("qT_scr", [D, S], BF16)
        self.kT = self.dscr("kT_scr", [D, S], BF16)
        self.v = self.dscr("v_scr", [S, D], BF16)
        self.g1T = self.dscr("g1T_scr", [D, S])
        self.gtok = self.dscr("gtok_scr", [S, D])
        self.xaT = self.dscr("xaT_scr", [512, S])
        self.mixT = self.dscr("mixT_scr", [D, S], BF16)
        self.mixtok = self.dscr("mixtok_scr", [S, D], BF16)
        self.rope = self.dscr("rope_scr", [4, 128, S])

        with ExitStack() as es:
            self.K = Kern(nc, es)
            K = self.K
            es.enter_context(nc.Block())
            pq = [es.enter_context(nc.psum_tensor("pq%d" % i, [128, 1024], F32)) for i in range(4)]
            self.ps = [Buf(pq[i // 2][:, (i % 2) * 512:(i % 2) * 512 + 512], "ps%d" % i) for i in range(6)]
            self.pb = [Buf(pq[3][:, i * 512:(i + 1) * 512].bitcast(BF16), "pb%d" % i) for i in range(2)]
            self.pq2 = [Buf(pq[i], "pq%d" % i) for i in range(3)]
            self.po = [Buf(pq[3][:, i * 512:(i + 1) * 512], "po%d" % i) for i in range(2)]
            self.ident = self.sb(es, "ident", [128, 128], BF16)
            self.consts_phase(es)
            self.rope_phase()
            for li in range(self.n_layers):
                src = self.x if li == 0 else self.h
                if li % 2 == 0:
                    self.inproj_even(li, src)
                    self.lru_phase(li)
                    self.sb_phase(li)
                else:
                    self.inproj_odd(li, src)
                    self.diff_phase(li)
                self.out_phase(li, src)
            self.final_phase(self.x if self.n_layers == 0 else self.h)
        return nc

    def rms_scale(self, ss, rstd, n, nfeat):
        K = self.K
        K.op("dve", lambda e: e.tensor_scalar(out=rstd[:, 0:n], in0=ss[:, 0:n], scalar1=1.0 / nfeat,
                                              scalar2=EPS, op0=ALU.mult, op1=ALU.add),
             reads=[ss], writes=[rstd])
        K.op("act", lambda e: e.activation(out=rstd[:, 0:n], in_=rstd[:, 0:n], func=AF.Ln),
             reads=[rstd], writes=[rstd])
        K.op("act", lambda e: e.activation(out=rstd[:, 0:n], in_=rstd[:, 0:n], func=AF.Exp, scale=-0.5),
             reads=[rstd], writes=[rstd])

    def consts_phase(self, es):
        K = self.K
        with ExitStack() as ts:
            stg = self.sb(ts, "cst_stg", [128, 128])
            K.dma(stg[:], self.c_ident, stg, True)
            K.op("dve", lambda e: e.tensor_copy(out=self.ident[:], in_=stg[:]), reads=[stg], writes=[self.ident])
            K.barrier()

    def rope_phase(self):
        K = self.K
        CH = S // 4
        with ExitStack() as ts:
            posi = self.sb(ts, "posi", [128, CH], I32)
            ang = self.sb(ts, "ang", [128, CH])
            kf = self.sb(ts, "kf", [128, CH])
            ki = self.sb(ts, "ki", [128, CH], I32)
            r = self.sb(ts, "r", [128, CH])
            t = self.sb(ts, "t", [128, CH])
            y = self.sb(ts, "y", [128, CH])
            fr = self.sb(ts, "fr", [128, 2])
            for c in range(4):
                K.dma(posi[c * 32:(c + 1) * 32, :], self.pos[0:1, c * CH:(c + 1) * CH].broadcast_to([32, CH]), posi, True)
            K.dma(fr[:], self.c_freq, fr, True)
            K.op("dve", lambda e: e.tensor_copy(out=ang[:], in_=posi[:]), reads=[posi], writes=[ang])
            K.op("dve", lambda e: e.tensor_scalar(out=ang[:], in0=ang[:], scalar1=fr[:, 0:1], scalar2=None,
                                                  op0=ALU.mult), reads=[ang, fr], writes=[ang])
            K.op("dve", lambda e: e.tensor_scalar(out=kf[:], in0=ang[:], scalar1=1.0 / TWO_PI, scalar2=None,
                                                  op0=ALU.mult), reads=[ang], writes=[kf])
            K.op("dve", lambda e: e.tensor_copy(out=ki[:], in_=kf[:]), reads=[kf], writes=[ki])
            K.op("dve", lambda e: e.tensor_copy(out=kf[:], in_=ki[:]), reads=[ki], writes=[kf])
            K.op("dve", lambda e: e.scalar_tensor_tensor(out=r[:], in0=kf[:], scalar=-CW1, in1=ang[:],
                                                         op0=ALU.mult, op1=ALU.add), reads=[kf, ang], writes=[r])
            K.op("dve", lambda e: e.scalar_tensor_tensor(out=r[:], in0=kf[:], scalar=-CW2, in1=r[:],
                                                         op0=ALU.mult, op1=ALU.add), reads=[kf, r], writes=[r])

            def wrap(dst, src, shift):
                K.op("dve", lambda e: e.tensor_scalar(out=dst[:], in0=src[:], scalar1=shift, scalar2=None,
                                                      op0=ALU.add), reads=[src], writes=[dst])
                K.op("dve", lambda e: e.tensor_scalar(out=t[:], in0=dst[:], scalar1=math.pi, scalar2=-TWO_PI,
                                                      op0=ALU.is_gt, op1=ALU.mult), reads=[dst], writes=[t])
                K.op("dve", lambda e: e.tensor_tensor(out=dst[:], in0=dst[:], in1=t[:], op=ALU.add),
                     reads=[dst, t], writes=[dst])
                K.op("dve", lambda e: e.tensor_scalar(out=t[:], in0=dst[:], scalar1=-math.pi, scalar2=TWO_PI,
                                                      op0=ALU.is_lt, op1=ALU.mult), reads=[dst], writes=[t])
                K.op("dve", lambda e: e.tensor_tensor(out=dst[:], in0=dst[:], in1=t[:], op=ALU.add),
                     reads=[dst, t], writes=[dst])
                K.op("dve", lambda e: e.tensor_scalar(out=dst[:], in0=dst[:], scalar1=-3.1415925, scalar2=3.1415925,
                                                      op0=ALU.max, op1=ALU.min), reads=[dst], writes=[dst])

            def emit(tile, idx, reps):
                for rep in reps:
                    for c in range(4):
                        K.dma(self.rope[idx, rep * 32:(rep + 1) * 32, c * CH:(c + 1) * CH], tile[c * 32:(c + 1) * 32, :],
                              tile, False)

            for which, shift in ((1, 0.0), (0, math.pi / 2)):
                o = self.sb(ts, "ro%d" % which, [128, CH])
                o8 = self.sb(ts, "ro8_%d" % which, [128, CH])
                wrap(y, r, shift)
                K.op("act", lambda e: e.activation(out=o[:], in_=y[:], func=AF.Sin), reads=[y], writes=[o])
                K.op("pool", lambda e: e.tensor_scalar(out=o8[:], in0=o[:], scalar1=0.125, scalar2=None,
                                                       op0=ALU.mult), reads=[o], writes=[o8])
                if which == 1:
                    on_ = self.sb(ts, "ron", [128, CH])
                    on8 = self.sb(ts, "ron8", [128, CH])
                    K.op("pool", lambda e: e.tensor_scalar(out=on_[:], in0=o[:], scalar1=-1.0, scalar2=None,
                                                           op0=ALU.mult), reads=[o], writes=[on_])
                    K.op("pool", lambda e: e.tensor_scalar(out=on8[:], in0=o[:], scalar1=-0.125, scalar2=None,
                                                           op0=ALU.mult), reads=[o], writes=[on8])
                    emit(o, 3, (1, 3))
                    emit(on_, 3, (0, 2))
                    emit(o8, 1, (1, 3))
                    emit(on8, 1, (0, 2))
                else:
                    emit(o, 2, (0, 1, 2, 3))
                    emit(o8, 0, (0, 1, 2, 3))
            K.barrier()

    def inproj(self, src, gain_row, wsrc, col0, ncols, fm_specs, tm_specs, rope_tabs=False, perm=False):
        K = self.K
        with ExitStack() as ts:
            w = self.sb(ts, "w_in", [128, 8, ncols], BF16)
            nstg = 5
            stgs = [self.sb(ts, "w_stg%d" % i, [128, 1024]) for i in range(nstg)]
            permT = None
            if perm:
                permT = self.sb(ts, "permT", [128, 128], BF16)
                K.dma(stgs[0][:, 0:128], self.c_perm, stgs[0], True)
                K.op("dve", lambda e: e.tensor_copy(out=permT[:], in_=stgs[0][:, 0:128]), reads=[stgs[0]], writes=[permT])
            n = 0
            for kc in range(8):
                for c0 in range(0, ncols, 1024):
                    cw = min(1024, ncols - c0)
                    stg = stgs[n % nstg]
                    n += 1
                    K.dma(stg[:, 0:cw], wsrc[kc * 128:(kc + 1) * 128, col0 + c0:col0 + c0 + cw], stg, True)
                    if n % 2 == 0:
                        K.op("act", lambda e: e.activation(out=w[:, kc, c0:c0 + cw], in_=stg[:, 0:cw], func=AF.Copy),
                             reads=[stg], writes=[w])
                    else:
                        K.op("dve", lambda e: e.tensor_copy(out=w[:, kc, c0:c0 + cw], in_=stg[:, 0:cw]),
                             reads=[stg], writes=[w])
            gbc = self.sb(ts, "gbc", [128, D])
            K.dma(gbc[:], gain_row.broadcast_to([128, D]), gbc, True)
            hx = [self.sb(ts, "hx%d" % i, [128, 4, D]) for i in range(2)]
            hn = self.sb(ts, "hn", [128, 4, D], BF16)
            hnT = [self.sb(ts, "hnT%d" % i, [128, 8, 512], BF16) for i in range(2)]
            junk = self.sb(ts, "junk", [128, D])
            ss = [self.sb(ts, "ss%d" % i, [128, 4]) for i in range(2)]
            rstd = [self.sb(ts, "rstd%d" % i, [128, 4]) for i in range(2)]
            o32 = [self.sb(ts, "o32_%d" % i, [128, 512]) for i in range(3)]
            o16 = [self.sb(ts, "o16_%d" % i, [128, 512], BF16) for i in range(3)]
            if rope_tabs:
                rt = [self.sb(ts, "rt%d" % i, [128, 4, 512]) for i in range(2)]
                xbs = [self.sb(ts, "xb_%d" % i, [128, 512], BF16) for i in range(3)]
                t1 = [self.sb(ts, "t1_%d" % i, [128, 512]) for i in range(2)]
                t2 = [self.sb(ts, "t2_%d" % i, [128, 512]) for i in range(2)]
            cnt = {"ps": 0, "o32": 0, "o16": 0, "cp": 0, "xb": 0}

            def next_ps():
                b = self.ps[cnt["ps"] % 6]
                cnt["ps"] += 1
                return b

            def next_o32():
                b = o32[cnt["o32"] % 3]
                cnt["o32"] += 1
                return b

            def next_o16():
                b = o16[cnt["o16"] % 3]
                cnt["o16"] += 1
                return b

            def evac(dst, ps, kind):
                if kind == "silu":
                    K.op("act", lambda e: e.activation(out=dst[:], in_=ps[:], func=AF.Silu), reads=[ps], writes=[dst])
                elif kind == "scale8":
                    K.op("act", lambda e: e.activation(out=dst[:], in_=ps[:], func=AF.Copy, scale=0.125),
                         reads=[ps], writes=[dst])
                else:
                    cnt["cp"] += 1
                    if cnt["cp"] % 2 == 0:
                        K.op("dve", lambda e: e.tensor_copy(out=dst[:], in_=ps[:]), reads=[ps], writes=[dst])
                    else:
                        K.op("act", lambda e: e.activation(out=dst[:], in_=ps[:], func=AF.Copy),
                             reads=[ps], writes=[dst])

            hns = [hn, self.sb(ts, "hn_b", [128, 4, D], BF16)]

            def N1(tb):
                hxb = hx[tb % 2]
                ssb, rsb = ss[tb % 2], rstd[tb % 2]
                hnb = hns[tb % 2]
                K.dma(hxb[:], src[tb * 512:(tb + 1) * 512, :].rearrange("(j p) d -> p j d", p=128), hxb, True)
                for jj in range(4):
                    K.op("act", lambda e: e.activation(out=junk[:], in_=hxb[:, jj, :], func=AF.Square,
                                                       accum_out=ssb[:, jj:jj + 1]),
                         reads=[hxb], writes=[junk, ssb])
                self.rms_scale(ssb, rsb, 4, D)
                for jj in range(4):
                    K.op("dve", lambda e: e.scalar_tensor_tensor(out=hnb[:, jj, :], in0=hxb[:, jj, :],
                                                                 scalar=rsb[:, jj:jj + 1], in1=gbc[:],
                                                                 op0=ALU.mult, op1=ALU.mult),
                         reads=[hxb, rsb, gbc], writes=[hnb])

            def N2(tb):
                hT = hnT[tb % 2]
                hnb = hns[tb % 2]
                if rope_tabs:
                    rtb = rt[tb % 2]
                    K.dma(rtb[:], self.rope[:, :, tb * 512:(tb + 1) * 512].rearrange("f p t -> p f t"), rtb, True)
                for kc in range(8):
                    pb = self.pb[kc % 2]
                    for jj in range(4):
                        K.op("pe", lambda e: e.transpose(out=pb[:, jj * 128:(jj + 1) * 128],
                                                         in_=hnb[:, jj, kc * 128:(kc + 1) * 128],
                                                         identity=self.ident[:]),
                             reads=[hnb, self.ident], writes=[pb], sig=(jj == 3))
                    if kc % 2 == 0:
                        K.op("act", lambda e: e.activation(out=hT[:, kc, :], in_=pb[:, 0:512], func=AF.Copy),
                             reads=[pb], writes=[hT])
                    else:
                        K.op("dve", lambda e: e.tensor_copy(out=hT[:, kc, :], in_=pb[:, 0:512]),
                             reads=[pb], writes=[hT])

            def M(tb):
                hT = hnT[tb % 2]
                rtb = rt[tb % 2] if rope_tabs else None
                pend = [None]

                def rope_tail():
                    if pend[0] is None:
                        return
                    ps_, xb_, kind_, dstf_ = pend[0]
                    pend[0] = None
                    ps2 = next_ps()
                    K.op("pe", lambda e: e.matmul(ps2[:], lhsT=permT[:], rhs=xb_[:], start=True, stop=True),
                         reads=[permT, xb_], writes=[ps2])
                    ci, si = (0, 1) if kind_ == "ropeq" else (2, 3)
                    a_, b_ = t1[cnt["o16"] % 2], t2[cnt["o16"] % 2]
                    K.op("dve", lambda e: e.tensor_tensor(out=a_[:], in0=ps_[:], in1=rtb[:, ci, :], op=ALU.mult),
                         reads=[ps_, rtb], writes=[a_])
                    K.op("dve", lambda e: e.tensor_tensor(out=b_[:], in0=ps2[:], in1=rtb[:, si, :], op=ALU.mult),
                         reads=[ps2, rtb], writes=[b_])
                    ob = next_o16()
                    K.op("pool", lambda e: e.tensor_tensor(out=ob[:], in0=a_[:], in1=b_[:], op=ALU.add),
                         reads=[a_, b_], writes=[ob])
                    K.dma(dstf_(tb), ob[:], ob, False)

                def fm_tile(ft, kind, dstf):
                    ps = next_ps()
                    for kc in range(8):
                        K.op("pe", lambda e: e.matmul(ps[:], lhsT=w[:, kc, ft * 128:(ft + 1) * 128], rhs=hT[:, kc, :],
                                                      start=(kc == 0), stop=(kc == 7)),
                             reads=[w, hT], writes=[ps], sig=(kc == 7))
                    if kind in ("ropeq", "ropek"):
                        xb = xbs[cnt["xb"] % 3]
                        cnt["xb"] += 1
                        K.op("dve", lambda e: e.tensor_copy(out=xb[:], in_=ps[:]), reads=[ps], writes=[xb])
                        rope_tail()
                        pend[0] = (ps, xb, kind, dstf)
                    elif kind in ("f32", "silu"):
                        rope_tail()
                        ob = next_o32()
                        evac(ob, ps, kind)
                        K.dma(dstf(tb), ob[:], ob, False)
                    else:
                        rope_tail()
                        ob = next_o16()
                        evac(ob, ps, kind)
                        K.dma(dstf(tb), ob[:], ob, False)

                def tm_tile(c0, kind, dstf, jj):
                    ps = next_ps()
                    for kc in range(8):
                        K.op("pe", lambda e: e.matmul(ps[:], lhsT=hT[:, kc, jj * 128:(jj + 1) * 128],
                                                      rhs=w[:, kc, c0:c0 + 512], start=(kc == 0), stop=(kc == 7)),
                             reads=[w, hT], writes=[ps], sig=(kc == 7))
                    if kind == "silu":
                        ob = next_o32()
                    else:
                        ob = next_o16()
                    evac(ob, ps, kind)
                    K.dma(dstf(tb * 512 + jj * 128), ob[:], ob, False)

                fm_jobs = [(lambda a=a_: fm_tile(*a)) for a_ in fm_specs]
                tm_jobs = [(lambda a=(c0, kind, dstf, jj): tm_tile(*a)) for (c0, kind, dstf) in tm_specs for jj in range(4)]
                nf, nt = len(fm_jobs), len(tm_jobs)
                fi = ti = 0
                while fi < nf or ti < nt:
                    if fi < nf:
                        fm_jobs[fi]()
                        fi += 1
                    while ti < nt and (fi >= nf or ti * nf < fi * nt):
                        tm_jobs[ti]()
                        ti += 1
                rope_tail()

            N1(0)
            N2(0)
            N1(1)
            for tb in range(8):
                if tb + 1 < 8:
                    N2(tb + 1)
                if tb + 2 < 8:
                    N1(tb + 2)
                M(tb)
            K.barrier()

    def inproj_even(self, li, src):
        j = li // 2

        def rows(t, r0):
            return lambda tb: t[r0:r0 + 128, tb * 512:(tb + 1) * 512]

        fm = []
        for i in range(4):
            fm.append((i, "f32", rows(self.xaT, i * 128)))
        for i in range(4):
            fm.append((4 + i, "silu", rows(self.g1T, i * 128)))
        for i in range(4):
            fm.append((8 + i, "scale8", rows(self.qT, i * 128)))
        for i in range(4):
            fm.append((12 + i, "bf16", rows(self.kT, i * 128)))
        for i in range(4):
            fm.append((20 + i, "silu", rows(self.g1T, 512 + i * 128)))
        tm = [(2048, "bf16", lambda t0: self.v[t0:t0 + 128, 0:512])]
        self.inproj(src, self.norm_mix[li:li + 1, :], self.w_in_e[j], 0, 3072, fm, tm)

    def inproj_odd(self, li, src):
        j = li // 2

        def rows(t, r0):
            return lambda tb: t[r0:r0 + 128, tb * 512:(tb + 1) * 512]

        fm = []
        for i in range(8):
            fm.append((i, "ropeq", rows(self.qT, i * 128)))
        for i in range(8):
            fm.append((8 + i, "ropek", rows(self.kT, i * 128)))
        tm = []
        for c in range(2):
            tm.append((2048 + c * 512, "bf16", (lambda c: lambda t0: self.v[t0:t0 + 128, c * 512:(c + 1) * 512])(c)))
        for c in range(2):
            tm.append((3072 + c * 512, "silu",
                       (lambda c: lambda t0: self.gtok[t0:t0 + 128, c * 512:(c + 1) * 512])(c)))
        self.inproj(src, self.norm_mix[li:li + 1, :], self.w_in_o[j], 0, 4096, fm, tm, rope_tabs=True, perm=True)

    def lru_phase(self, li):
        K = self.K
        j = li // 2
        with ExitStack() as ts:
            pv = self.sb(ts, "pv", [128, 4, 8])
            K.dma(pv[:], self.pvec_e[j], pv, True)
            c1 = self.sb(ts, "c1", [128, 4])
            K.op("act", lambda e: e.activation(out=c1[:], in_=pv[:, :, 7], func=AF.Exp, scale=-1.0),
                 reads=[pv], writes=[c1])
            K.op("act", lambda e: e.activation(out=c1[:], in_=c1[:], func=AF.Ln, bias=1.0), reads=[c1], writes=[c1])
            K.op("dve", lambda e: e.tensor_scalar(out=c1[:], in0=c1[:], scalar1=-8.0, scalar2=None, op0=ALU.mult),
                 reads=[c1], writes=[c1])
            wst = self.sb(ts, "wst", [128, 128])
            wbd = [self.sb(ts, "wbd%d" % i, [128, 4, 128], BF16) for i in range(2)]
            for wi, wsrc in enumerate((self.lru_wa, self.lru_wx)):
                for ct in range(4):
                    K.op("pool", lambda e: e.memset(wst[:], 0.0), writes=[wst])
                    K.dma(wst[0:64, 0:64], wsrc[j, 2 * ct], wst, True)
                    K.dma(wst[64:128, 64:128], wsrc[j, 2 * ct + 1], wst, True)
                    K.op("pool", lambda e: e.tensor_copy(out=wbd[wi][:, ct, :], in_=wst[:]),
                         reads=[wst], writes=[wbd[wi]])
            CH = 1024
            NCH = S // CH
            xp = [self.sb(ts, "xp%d" % i, [128, CH + 3]) for i in range(3)]
            xc = [self.sb(ts, "xc%d" % i, [128, CH]) for i in range(2)]
            xcb = [self.sb(ts, "xcb%d" % i, [128, CH], BF16) for i in range(2)]
            r = [self.sb(ts, "lr%d" % i, [128, CH]) for i in range(3)]
            ii = [self.sb(ts, "li%d" % i, [128, CH]) for i in range(3)]
            a2 = [self.sb(ts, "la2%d" % i, [128, CH]) for i in range(2)]
            hh = [self.sb(ts, "lhh%d" % i, [128, CH]) for i in range(2)]
            ga = [self.sb(ts, "lga%d" % i, [128, CH]) for i in range(3)]
            ob = [self.sb(ts, "lob%d" % i, [128, CH], BF16) for i in range(2)]
            its = [(ct, ch) for ct in range(4) for ch in range(NCH)]
            cnt = [0]

            def LD(it):
                ct, ch = its[it]
                b = it % 3
                c0 = ch * CH
                rows = slice(ct * 128, (ct + 1) * 128)
                if ch == 0:
                    K.op("pool", lambda e: e.memset(xp[b][:, 0:3], 0.0), writes=[xp[b]])
                    K.dma(xp[b][:, 3:CH + 3], self.xaT[rows, 0:CH], xp[b], True)
                else:
                    K.dma(xp[b][:], self.xaT[rows, c0 - 3:c0 + CH], xp[b], True)
                K.dma(ga[b][:], self.g1T[rows, c0:c0 + CH], ga[b], True)

            def P(it):
                ct, ch = its[it]
                b = it % 3
                b2 = it % 2
                c0 = ch * CH
                rows = slice(ct * 128, (ct + 1) * 128)
                K.op("dve", lambda e: e.tensor_scalar(out=xc[b2][:], in0=xp[b][:, 0:CH], scalar1=pv[:, ct, 0:1],
                                                      scalar2=pv[:, ct, 4:5], op0=ALU.mult, op1=ALU.add),
                     reads=[xp[b], pv], writes=[xc[b2]])
                for k in range(1, 4):
                    K.op("dve", lambda e: e.scalar_tensor_tensor(out=xc[b2][:], in0=xp[b][:, k:k + CH],
                                                                 scalar=pv[:, ct, k:k + 1], in1=xc[b2][:],
                                                                 op0=ALU.mult, op1=ALU.add),
                         reads=[xp[b], pv, xc[b2]], writes=[xc[b2]])
                K.op("act", lambda e: e.activation(out=xcb[b2][:], in_=xc[b2][:], func=AF.Copy),
                     reads=[xc[b2]], writes=[xcb[b2]])
                for blk in range(CH // 512):
                    for wi, dst, bcol in ((0, r[b], 5), (1, ii[b], 6)):
                        ps = self.ps[cnt[0] % 4]
                        cnt[0] += 1
                        K.op("pe", lambda e: e.matmul(ps[:], lhsT=wbd[wi][:, ct, :], rhs=xcb[b2][:, blk * 512:(blk + 1) * 512],
                                                      start=True, stop=True), reads=[wbd[wi], xcb[b2]], writes=[ps])
                        K.op("act", lambda e: e.activation(out=dst[:, blk * 512:(blk + 1) * 512], in_=ps[:],
                                                           func=AF.Sigmoid, bias=pv[:, ct, bcol:bcol + 1]),
                             reads=[ps, pv], writes=[dst])

            def P1b(it):
                b = it % 3
                b2 = it % 2
                K.op("dve", lambda e: e.tensor_tensor(out=ii[b][:], in0=ii[b][:], in1=xc[b2][:], op=ALU.mult),
                     reads=[ii[b], xc[b2]], writes=[ii[b]])

            def P2(it):
                ct, ch = its[it]
                b = it % 3
                b2 = it % 2
                K.op("act", lambda e: e.activation(out=r[b][:], in_=r[b][:], func=AF.Exp, scale=c1[:, ct:ct + 1]),
                     reads=[r[b], c1], writes=[r[b]])
                K.op("act", lambda e: e.activation(out=a2[b2][:], in_=r[b][:], func=AF.Square), reads=[r[b]], writes=[a2[b2]])
                K.op("act", lambda e: e.activation(out=a2[b2][:], in_=a2[b2][:], func=AF.Sqrt, scale=-1.0, bias=1.0),
                     reads=[a2[b2]], writes=[a2[b2]])
                K.op("pool", lambda e: e.tensor_tensor(out=ii[b][:], in0=ii[b][:], in1=a2[b2][:], op=ALU.mult),
                     reads=[ii[b], a2[b2]], writes=[ii[b]])

            def Q(it):
                ct, ch = its[it]
                b = it % 3
                b2 = it % 2
                c0 = ch * CH
                rows = slice(ct * 128, (ct + 1) * 128)
                if ch == 0:
                    K.op("dve", lambda e: e.tensor_tensor_scan(out=hh[b2][:], data0=r[b][:], data1=ii[b][:], initial=0.0,
                                                               op0=ALU.mult, op1=ALU.add),
                         reads=[r[b], ii[b]], writes=[hh[b2]])
                else:
                    K.op("dve", lambda e: e.tensor_tensor_scan(out=hh[b2][:], data0=r[b][:], data1=ii[b][:],
                                                               initial=hh[(it - 1) % 2][:, CH - 1:CH],
                                                               op0=ALU.mult, op1=ALU.add),
                         reads=[r[b], ii[b], hh[(it - 1) % 2]], writes=[hh[b2]])
                K.op("dve", lambda e: e.tensor_tensor(out=ob[b2][:], in0=hh[b2][:], in1=ga[b][:], op=ALU.mult),
                     reads=[hh[b2], ga[b]], writes=[ob[b2]])
                K.dma(self.mixT[rows, c0:c0 + CH], ob[b2][:], ob[b2], False)

            LD(0)
            LD(1)
            LD(2)
            P(0)
            P1b(0)
            P(1)
            P1b(1)
            P2(0)
            for it in range(len(its)):
                if it + 2 < len(its):
                    P(it + 2)
                Q(it)
                if it + 3 < len(its):
                    LD(it + 3)
                if it + 2 < len(its):
                    P1b(it + 2)
                if it + 1 < len(its):
                    P2(it + 1)
            K.barrier()

    def sb_phase(self, li):
        K = self.K
        with ExitStack() as ts:
            stg = self.sb(ts, "sb_stg", [128, 4, 512])
            ntinc = self.sb(ts, "ntinc", [128, 128], BF16)
            nones = self.sb(ts, "nones", [128, 128], BF16)
            masks = self.sb(ts, "masks", [128, 4, 512], BF16)
            K.dma(stg[:, 0, 0:128], self.c_ntinc, stg, True)
            K.op("dve", lambda e: e.tensor_copy(out=ntinc[:], in_=stg[:, 0, 0:128]), reads=[stg], writes=[ntinc])
            K.op("pool", lambda e: e.memset(nones[:], -1.0), writes=[nones])
            K.dma(stg[:], self.c_mask_s, stg, True)
            K.op("dve", lambda e: e.tensor_copy(out=masks[:], in_=stg[:]), reads=[stg], writes=[masks])
            kTs = [self.sb(ts, "kT%d" % i, [128, S], BF16) for i in range(2)]
            qTs = [self.sb(ts, "qT%d" % i, [128, S], BF16) for i in range(2)]
            vvs = [self.sb(ts, "vv%d" % i, [128, NT, 128], BF16) for i in range(2)]
            e32 = [self.sb(ts, "e32_%d" % i, [128, 1024]) for i in range(2)]
            spb = [self.sb(ts, "spb%d" % i, [128, 1024], BF16) for i in range(2)]
            ssum = [self.sb(ts, "ssum%d" % i, [128, 512]) for i in range(2)]
            NSB = 6
            ssb = [self.sb(ts, "ssb%d" % i, [128, 512], BF16) for i in range(NSB)]
            wb2 = [self.sb(ts, "wb2_%d" % i, [128, 1024], BF16) for i in range(2)]
            gbt = [self.sb(ts, "gbt%d" % i, [64, 512]) for i in range(2)]
            obs = [self.sb(ts, "obs%d" % i, [64, 512], BF16) for i in range(2)]
            QA = self.pq2
            QO = self.po

            tiles = []
            grp = 0
            for hp in range(4):
                for hh in range(2):
                    for qb in range(8):
                        nk = 4 * qb + 4
                        for i, kt in enumerate(reversed(range(nk))):
                            tiles.append(dict(hp=hp, hh=hh, qb=qb, kt=kt, first=(i == 0), last=(i == nk - 1),
                                              grp=grp, d=kt - 4 * qb, idx=i))
                        grp += 1
            N = len(tiles)
            NP = N // 2
            loaded = set()

            def ensure_loaded(hp):
                if hp in loaded:
                    return
                loaded.add(hp)
                K.dma(kTs[hp % 2][:], self.kT[hp * 128:(hp + 1) * 128, :], kTs[hp % 2], True)
                K.dma(qTs[hp % 2][:], self.qT[hp * 128:(hp + 1) * 128, :], qTs[hp % 2], True)
                K.dma(vvs[hp % 2][:], self.v[:, hp * 128:(hp + 1) * 128].rearrange("(t p) c -> p t c", p=128),
                      vvs[hp % 2], True)

            def s1(p):
                A = QA[p % 3]
                eb = e32[p % 2]
                sp2 = spb[p % 2]
                for h_ in range(2):
                    n = 2 * p + h_
                    t = tiles[n]
                    hp, hh, qb, kt, d = t["hp"], t["hh"], t["qb"], t["kt"], t["d"]
                    ensure_loaded(hp)
                    kT, qT = kTs[hp % 2], qTs[hp % 2]
                    pl, ph = hh * 64, hh * 64 + 64
                    cs = slice(h_ * 512, (h_ + 1) * 512)
                    K.op("pe", lambda e: e.matmul(A[:, cs], lhsT=kT[pl:ph, kt * 128:(kt + 1) * 128],
                                                  rhs=qT[pl:ph, qb * 512:(qb + 1) * 512], start=True, stop=False,
                                                  skip_group_check=True),
                         reads=[kT, qT], writes=[A], sig=(d < 0 and h_ == 1))
                    if d >= 0:
                        K.op("pe", lambda e: e.matmul(A[:, cs], lhsT=self.ident[:], rhs=masks[:, d, :], start=False,
                                                      stop=False, skip_group_check=True),
                             reads=[self.ident, masks], writes=[A], sig=(h_ == 1))
                K.op("act", lambda e: e.activation(out=eb[:], in_=A[:], func=AF.Exp), reads=[A], writes=[eb])
                K.op("act", lambda e: e.activation(out=sp2[:], in_=eb[:], func=AF.Ln, bias=1.0), reads=[eb], writes=[sp2])
                for h_ in range(2):
                    n = 2 * p + h_
                    t = tiles[n]
                    cs = slice(h_ * 512, (h_ + 1) * 512)
                    if not t["last"]:
                        i = t["idx"]
                        sn = ssb[(n + 1) % NSB]
                        if t["first"]:
                            K.op("dve", lambda e: e.tensor_copy(out=ssum[i % 2][:], in_=sp2[:, cs]), reads=[sp2],
                                 writes=[ssum[i % 2]])
                            K.op("pool", lambda e: e.tensor_copy(out=sn[:], in_=sp2[:, cs]), reads=[sp2], writes=[sn])
                        else:
                            K.op("dve", lambda e: e.tensor_tensor(out=ssum[i % 2][:], in0=ssum[(i - 1) % 2][:],
                                                                  in1=sp2[:, cs], op=ALU.add),
                                 reads=[ssum[(i - 1) % 2], sp2], writes=[ssum[i % 2]])
                            K.op("dve", lambda e: e.tensor_copy(out=sn[:], in_=ssum[i % 2][:]), reads=[ssum[i % 2]],
                                 writes=[sn])

            def s2(p):
                A = QA[p % 3]
                sp2 = spb[p % 2]
                for h_ in range(2):
                    n = 2 * p + h_
                    t = tiles[n]
                    cs = slice(h_ * 512, (h_ + 1) * 512)
                    K.op("pe", lambda e: e.matmul(A[:, cs], lhsT=ntinc[:], rhs=sp2[:, cs], start=False, stop=t["first"],
                                                  skip_group_check=True),
                         reads=[ntinc, sp2], writes=[A], sig=(t["first"] and h_ == 1))
                    if not t["first"]:
                        sc = ssb[n % NSB]
                        K.op("pe", lambda e: e.matmul(A[:, cs], lhsT=nones[:], rhs=sc[:], start=False, stop=True,
                                                      skip_group_check=True),
                             reads=[nones, sc], writes=[A], sig=(h_ == 1))
                w2 = wb2[p % 2]
                K.op("act", lambda e: e.activation(out=w2[:], in_=A[:], func=AF.Exp), reads=[A], writes=[w2])

            def s3(n):
                t = tiles[n]
                hp, hh, qb, kt = t["hp"], t["hh"], t["qb"], t["kt"]
                vv = vvs[hp % 2]
                O = QO[t["grp"] % 2]
                w2 = wb2[(n // 2) % 2]
                cs = slice((n % 2) * 512, (n % 2) * 512 + 512)
                K.op("pe", lambda e: e.matmul(O[0:64, :], lhsT=vv[:, kt, hh * 64:hh * 64 + 64], rhs=w2[:, cs],
                                              start=t["first"], stop=t["last"]),
                     reads=[vv, w2], writes=[O], sig=True)
                if t["last"]:
                    g = gbt[t["grp"] % 2]
                    ob = obs[t["grp"] % 2]
                    r0 = 512 + hp * 128 + hh * 64
                    K.dma(g[:], self.g1T[r0:r0 + 64, qb * 512:(qb + 1) * 512], g, True)
                    K.op("dve", lambda e: e.tensor_tensor(out=ob[:], in0=O[0:64, :], in1=g[:], op=ALU.mult),
                         reads=[O, g], writes=[ob])
                    K.dma(self.mixT[r0:r0 + 64, qb * 512:(qb + 1) * 512], ob[:], ob, False)

            for p in range(NP + 2):
                if p < NP:
                    s1(p)
                if 0 <= p - 1 < NP:
                    s2(p - 1)
                if 0 <= p - 2 < NP:
                    s3(2 * (p - 2))
                    s3(2 * (p - 2) + 1)
            K.barrier()

    def diff_phase(self, li):
        K = self.K
        j = li // 2
        lam_init = 0.8 - 0.6 * math.exp(-0.3 * li)
        with ExitStack() as ts:
            stg = self.sb(ts, "df_stg", [128, 2, 256])
            masks = self.sb(ts, "dmasks", [128, 2, 256], BF16)
            K.dma(stg[:], self.c_mask_i, stg, True)
            K.op("dve", lambda e: e.tensor_copy(out=masks[:], in_=stg[:]), reads=[stg], writes=[masks])
            lv = self.sb(ts, "lv", [128, 256])
            lt = self.sb(ts, "lt", [128, 64])
            ls = self.sb(ts, "ls", [128, 2])
            neglam = self.sb(ts, "neglam", [128, 1])
            K.dma(lv[:], self.lamv[j].broadcast_to([128, 256]), lv, True)
            for i in range(2):
                K.op("dve", lambda e: e.scalar_tensor_tensor(out=lt[:], in0=lv[:, i * 128:i * 128 + 64], scalar=1.0,
                                                             in1=lv[:, i * 128 + 64:i * 128 + 128], op0=ALU.mult,
                                                             op1=ALU.mult, accum_out=ls[:, i:i + 1]),
                     reads=[lv], writes=[lt, ls])
            K.op("act", lambda e: e.activation(out=ls[:], in_=ls[:], func=AF.Exp), reads=[ls], writes=[ls])
            K.op("dve", lambda e: e.tensor_tensor(out=neglam[:], in0=ls[:, 1:2], in1=ls[:, 0:1], op=ALU.subtract),
                 reads=[ls], writes=[neglam])
            K.op("dve", lambda e: e.tensor_scalar(out=neglam[:], in0=neglam[:], scalar1=-lam_init, scalar2=None,
                                                  op0=ALU.add), reads=[neglam], writes=[neglam])

            kz = [[self.sb(ts, "dkz%d_%d" % (m, i), [128, S], BF16) for i in range(2)] for m in range(2)]
            for m in range(2):
                for i in range(2):
                    K.op("pool", lambda e: e.memset(kz[m][i][:], 0.0), writes=[kz[m][i]])
            qTs = [self.sb(ts, "dqT%d" % i, [128, S], BF16) for i in range(2)]
            vas = [self.sb(ts, "va%d" % i, [128, NT, 130], BF16) for i in range(2)]
            for i in range(2):
                K.op("pool", lambda e: e.memset(vas[i][:, :, 128:130], 1.0), writes=[vas[i]])
            Eb = [self.sb(ts, "Eb%d" % i, [128, 512], BF16) for i in range(3)]
            gts = [self.sb(ts, "gts%d" % i, [128, 2, 128]) for i in range(2)]
            rec = [self.sb(ts, "rec%d" % i, [128, 4]) for i in range(2)]
            tt = [self.sb(ts, "tt%d" % i, [128, 128]) for i in range(2)]
            oo = [self.sb(ts, "oo%d" % i, [128, 128]) for i in range(2)]
            jk = self.sb(ts, "jk", [128, 128])
            ssq = [self.sb(ts, "ssq%d" % i, [128, 1]) for i in range(2)]
            rsd = [self.sb(ts, "rsd%d" % i, [128, 1]) for i in range(2)]
            onb = [self.sb(ts, "onb%d" % i, [128, 128], BF16) for i in range(2)]
            oT = [self.sb(ts, "oT%d" % i, [128, 256], BF16) for i in range(2)]

            tiles = []
            grp = 0
            for h in range(8):
                for qb in range(16):
                    nk = 2 * qb + 2
                    for kt in range(nk):
                        tiles.append(dict(h=h, qb=qb, kt=kt, first=(kt == 0), last=(kt == nk - 1), grp=grp,
                                          d=kt - 2 * qb))
                    grp += 1
            N = len(tiles)
            loaded = set()

            def ensure_loaded(h):
                if h in loaded:
                    return
                loaded.add(h)
                for m in range(2):
                    K.dma(kz[m][h % 2][m * 64:m * 64 + 64, :], self.kT[h * 128 + m * 64:h * 128 + m * 64 + 64, :],
                          kz[m][h % 2], True)
                K.dma(qTs[h % 2][:], self.qT[h * 128:(h + 1) * 128, :], qTs[h % 2], True)
                K.dma(vas[h % 2][:, :, 0:128], self.v[:, h * 128:(h + 1) * 128].rearrange("(t p) c -> p t c", p=128),
                      vas[h % 2], True)

            E2 = [self.sb(ts, "E2_%d" % i, [128, 1024], BF16) for i in range(3)]
            QA = self.pq2
            XY = [(self.ps[4], self.ps[5]), (self.po[0], self.po[1])]

            def s1(p):
                A = QA[p % 2]
                for h_ in range(2):
                    n = 2 * p + h_
                    t = tiles[n]
                    h, qb, kt, d = t["h"], t["qb"], t["kt"], t["d"]
                    ensure_loaded(h)
                    qT = qTs[h % 2]
                    for m in range(2):
                        kT = kz[m][h % 2]
                        c0 = h_ * 512 + m * 256
                        K.op("pe", lambda e: e.matmul(A[:, c0:c0 + 256], lhsT=kT[:, kt * 128:(kt + 1) * 128],
                                                      rhs=qT[:, qb * 256:(qb + 1) * 256], start=(m == 0), stop=(d < 0),
                                                      skip_group_check=True),
                             reads=[kT, qT], writes=[A], sig=(d < 0 and m == 1 and h_ == 1))
                    if d >= 0:
                        for m in range(2):
                            c0 = h_ * 512 + m * 256
                            K.op("pe", lambda e: e.matmul(A[:, c0:c0 + 256], lhsT=self.ident[:], rhs=masks[:, d, :],
                                                          start=False, stop=True, skip_group_check=True),
                                 reads=[self.ident, masks], writes=[A], sig=(m == 1 and h_ == 1))
                E = E2[p % 3]
                K.op("act", lambda e: e.activation(out=E[:], in_=A[:], func=AF.Exp), reads=[A], writes=[E])

            def s2(n):
                t = tiles[n]
                h, qb, kt, d = t["h"], t["qb"], t["kt"], t["d"]
                va = vas[h % 2]
                E = E2[(n // 2) % 3]
                e0 = (n % 2) * 512
                X, Y = XY[t["grp"] % 2]
                subs = [s_ for s_ in range(2) if not (d >= 0 and s_ < d)]
                for m, acc in ((0, X), (1, Y)):
                    for s_ in subs:
                        K.op("pe", lambda e: e.matmul(acc[:, s_ * 256:s_ * 256 + 130],
                                                      lhsT=E[:, e0 + m * 256 + s_ * 128:e0 + m * 256 + (s_ + 1) * 128],
                                                      rhs=va[:, kt, :], start=(t["first"] and s_ == 0),
                                                      stop=t["last"], skip_group_check=True),
                             reads=[E, va], writes=[acc], sig=(s_ == subs[-1]))

            NB4 = 4
            e_gt = [self.sb(ts, "e_gt%d" % i, [128, 2, 128]) for i in range(NB4)]
            e_rc = [self.sb(ts, "e_rc%d" % i, [128, 2, 4]) for i in range(NB4)]
            e_tt = [self.sb(ts, "e_tt%d" % i, [128, 2, 128]) for i in range(NB4)]
            e_oo = [self.sb(ts, "e_oo%d" % i, [128, 2, 128]) for i in range(NB4)]
            e_sq = [self.sb(ts, "e_sq%d" % i, [128, 2]) for i in range(NB4)]
            e_rs = [self.sb(ts, "e_rs%d" % i, [128, 2]) for i in range(NB4)]
            e_on = [self.sb(ts, "e_on%d" % i, [128, 2, 128], BF16) for i in range(NB4)]
            e_oT = [self.sb(ts, "e_oT%d" % i, [128, 256], BF16) for i in range(NB4)]

            def epi1(n):
                t = tiles[n]
                h, qb, g = t["h"], t["qb"], t["grp"]
                X, Y = XY[g % 2]
                k4 = g % NB4
                gt, rc, tb_, ob_, sq, rs = e_gt[k4], e_rc[k4], e_tt[k4], e_oo[k4], e_sq[k4], e_rs[k4]
                K.dma(gt[:], self.gtok[qb * 256:(qb + 1) * 256, h * 128:(h + 1) * 128].rearrange("(s p) c -> p s c", p=128),
                      gt, True)
                for s_ in range(2):
                    K.op("dve", lambda e: e.reciprocal(out=rc[:, s_, 0:1], in_=X[:, s_ * 256 + 128:s_ * 256 + 129]),
                         reads=[X], writes=[rc])
                    K.op("dve", lambda e: e.reciprocal(out=rc[:, s_, 1:2], in_=Y[:, s_ * 256 + 128:s_ * 256 + 129]),
                         reads=[Y], writes=[rc])
                    K.op("dve", lambda e: e.tensor_scalar(out=rc[:, s_, 2:3], in0=rc[:, s_, 1:2], scalar1=neglam[:, 0:1],
                                                          scalar2=None, op0=ALU.mult), reads=[rc, neglam], writes=[rc])
                    K.op("dve", lambda e: e.tensor_scalar(out=tb_[:, s_, :], in0=Y[:, s_ * 256:s_ * 256 + 128],
                                                          scalar1=rc[:, s_, 2:3], scalar2=None, op0=ALU.mult),
                         reads=[Y, rc], writes=[tb_])
                    K.op("dve", lambda e: e.scalar_tensor_tensor(out=ob_[:, s_, :], in0=X[:, s_ * 256:s_ * 256 + 128],
                                                                 scalar=rc[:, s_, 0:1], in1=tb_[:, s_, :], op0=ALU.mult,
                                                                 op1=ALU.add), reads=[X, rc, tb_], writes=[ob_])
                    K.op("dve", lambda e: e.scalar_tensor_tensor(out=tb_[:, s_, :], in0=ob_[:, s_, :], scalar=1.0,
                                                                 in1=ob_[:, s_, :], op0=ALU.mult, op1=ALU.mult,
                                                                 accum_out=sq[:, s_:s_ + 1]),
                         reads=[ob_], writes=[tb_, sq])
                K.op("dve", lambda e: e.tensor_scalar(out=rs[:], in0=sq[:], scalar1=1.0 / 128, scalar2=EPS,
                                                      op0=ALU.mult, op1=ALU.add), reads=[sq], writes=[rs])
                K.op("pool", lambda e: e.tensor_tensor(out=ob_[:], in0=ob_[:], in1=gt[:], op=ALU.mult),
                     reads=[ob_, gt], writes=[ob_])

            def epi2(n):
                k4 = tiles[n]["grp"] % NB4
                rs = e_rs[k4]
                K.op("act", lambda e: e.activation(out=rs[:], in_=rs[:], func=AF.Ln), reads=[rs], writes=[rs])
                K.op("act", lambda e: e.activation(out=rs[:], in_=rs[:], func=AF.Exp, scale=-0.5), reads=[rs], writes=[rs])

            def epi3(n):
                t = tiles[n]
                h, qb = t["h"], t["qb"]
                k4 = t["grp"] % NB4
                ob_, rs, on = e_oo[k4], e_rs[k4], e_on[k4]
                for s_ in range(2):
                    K.op("dve", lambda e: e.tensor_scalar(out=on[:, s_, :], in0=ob_[:, s_, :], scalar1=rs[:, s_:s_ + 1],
                                                          scalar2=None, op0=ALU.mult), reads=[ob_, rs], writes=[on])
                K.dma(self.mixtok[qb * 256:(qb + 1) * 256, h * 128:(h + 1) * 128].rearrange("(s p) c -> p s c", p=128),
                      on[:], on, False)

            NP = N // 2
            pending = []
            for p in range(NP + 1):
                if p < NP:
                    s1(p)
                if 0 <= p - 1 < NP:
                    for n in (2 * (p - 1), 2 * (p - 1) + 1):
                        s2(n)
                        if tiles[n]["last"]:
                            pending.append((p, epi1, n))
                            pending.append((p + 1, epi2, n))
                            pending.append((p + 2, epi3, n))
                            pending.sort(key=lambda x: x[0])
                while pending and pending[0][0] <= p:
                    _, fn, arg = pending.pop(0)
                    fn(arg)
            while pending:
                _, fn, arg = pending.pop(0)
                fn(arg)
            K.barrier()

    def out_phase(self, li, src):
        K = self.K
        j = li // 2
        odd = (li % 2 == 1)
        lam_init = 0.8 - 0.6 * math.exp(-0.3 * li)
        with ExitStack() as ts:
            stgs = [self.sb(ts, "o_stg%d" % i, [128, 1024]) for i in range(3)]
            n = [0]

            def lw(name, wsrc, kcn, scale=None):
                w = self.sb(ts, name, [128, kcn, D], BF16)
                for kc in range(kcn):
                    stg = stgs[n[0] % 3]
                    eng = "act" if n[0] % 2 == 0 else "dve"
                    n[0] += 1
                    K.dma(stg[:], wsrc[kc * 128:(kc + 1) * 128, :], stg, True)
                    if scale is None and eng == "act":
                        K.op("act", lambda e: e.activation(out=w[:, kc, :], in_=stg[:], func=AF.Copy),
                             reads=[stg], writes=[w])
                    elif scale is None:
                        K.op(eng, lambda e: e.tensor_copy(out=w[:, kc, :], in_=stg[:]), reads=[stg], writes=[w])
                    else:
                        K.op("dve", lambda e: e.tensor_scalar(out=w[:, kc, :], in0=stg[:], scalar1=scale[:, 0:1],
                                                              scalar2=None, op0=ALU.mult),
                             reads=[stg, scale], writes=[w])
                return w

            scale = None
            if odd:
                sg = self.sb(ts, "sg", [128, 2])
                scale = self.sb(ts, "sgs", [128, 1])
                K.dma(sg[:], self.subg, sg, True)
                K.op("dve", lambda e: e.tensor_scalar(out=scale[:], in0=sg[:, j:j + 1], scalar1=(1.0 - lam_init),
                                                      scalar2=None, op0=ALU.mult), reads=[sg], writes=[scale])
            wo = lw("wo", (self.w_out_o if odd else self.w_out_e)[j], 8, scale)
            wg = lw("wg", self.w_ple_gate[li], 8)
            wp = lw("wp", self.w_ple_proj[li], 2)
            gbc = self.sb(ts, "gple", [128, D])
            K.dma(gbc[:], self.norm_ple[li:li + 1, :].broadcast_to([128, D]), gbc, True)
            mT = [self.sb(ts, "mT%d" % i, [128, 8, 128], BF16) for i in range(3)]
            hx = [self.sb(ts, "ohx%d" % i, [128, D]) for i in range(3)]
            pt = [self.sb(ts, "pt%d" % i, [128, 256]) for i in range(3)]
            ptb = [self.sb(ts, "ptb%d" % i, [128, 256], BF16) for i in range(2)]
            h1 = [self.sb(ts, "h1_%d" % i, [128, D]) for i in range(2)]
            junk = self.sb(ts, "ojunk", [128, D])
            ss = [self.sb(ts, "oss%d" % i, [128, 1]) for i in range(2)]
            rs = [self.sb(ts, "ors%d" % i, [128, 1]) for i in range(2)]
            hn2 = [self.sb(ts, "hn2_%d" % i, [128, D], BF16) for i in range(2)]
            hn2T = [self.sb(ts, "hn2T%d" % i, [128, 8, 128], BF16) for i in range(2)]
            pT = [self.sb(ts, "pT%d" % i, [128, 2, 128], BF16) for i in range(2)]
            gs = [self.sb(ts, "gs%d" % i, [128, D]) for i in range(2)]
            h2 = [self.sb(ts, "h2_%d" % i, [128, D]) for i in range(2)]
            mtok = [self.sb(ts, "mtok%d" % i, [128, D], BF16) for i in range(4)] if odd else None

            def LD(tt):
                b3 = tt % 3
                r0 = tt * 128
                if odd:
                    K.dma(mtok[tt % 4][:], self.mixtok[r0:r0 + 128, :], mtok[tt % 4], True)
                else:
                    K.dma(mT[b3][:], self.mixT[:, r0:r0 + 128].rearrange("(kc p) t -> p kc t", p=128), mT[b3], True)
                K.dma(hx[b3][:], src[r0:r0 + 128, :], hx[b3], True)
                K.dma(pt[b3][:], self.p[li, r0:r0 + 128, :], pt[b3], True)

            def TR(tt):
                b3 = tt % 3
                mk = mtok[tt % 4]
                pbm = self.pb[1]
                for kc in range(8):
                    K.op("pe", lambda e: e.transpose(out=pbm[:, kc * 128:(kc + 1) * 128],
                                                     in_=mk[:, kc * 128:(kc + 1) * 128], identity=self.ident[:]),
                         reads=[mk, self.ident], writes=[pbm], sig=(kc == 7))
                K.op("dve", lambda e: e.tensor_copy(out=mT[b3][:].rearrange("p k t -> p (k t)"), in_=pbm[:]),
                     reads=[pbm], writes=[mT[b3]])

            def A1(tt):
                b = tt % 2
                b3 = tt % 3
                for c in range(2):
                    ps = self.ps[c]
                    for kc in range(8):
                        K.op("pe", lambda e: e.matmul(ps[:], lhsT=mT[b3][:, kc, :], rhs=wo[:, kc, c * 512:(c + 1) * 512],
                                                      start=(kc == 0), stop=(kc == 7)),
                             reads=[mT[b3], wo], writes=[ps], sig=(kc == 7))
                    K.op("dve", lambda e: e.tensor_tensor(out=h1[b][:, c * 512:(c + 1) * 512], in0=ps[:],
                                                          in1=hx[b3][:, c * 512:(c + 1) * 512], op=ALU.add),
                         reads=[ps, hx[b3]], writes=[h1[b]])
                K.op("act", lambda e: e.activation(out=junk[:], in_=h1[b][:], func=AF.Square, accum_out=ss[b][:, 0:1]),
                     reads=[h1[b]], writes=[junk, ss[b]])
                self.rms_scale(ss[b], rs[b], 1, D)
                K.op("dve", lambda e: e.scalar_tensor_tensor(out=hn2[b][:], in0=h1[b][:], scalar=rs[b][:, 0:1],
                                                             in1=gbc[:], op0=ALU.mult, op1=ALU.mult),
                     reads=[h1[b], rs[b], gbc], writes=[hn2[b]])
                K.op("pool", lambda e: e.tensor_copy(out=ptb[b][:], in_=pt[b3][:]), reads=[pt[b3]], writes=[ptb[b]])

            def A2(tt):
                b = tt % 2
                pbk = self.pb[0]
                for kc in range(8):
                    K.op("pe", lambda e: e.transpose(out=pbk[:, kc * 128:(kc + 1) * 128], in_=hn2[b][:, kc * 128:(kc + 1) * 128],
                                                     identity=self.ident[:]),
                         reads=[hn2[b], self.ident], writes=[pbk], sig=(kc == 7))
                K.op("act", lambda e: e.activation(out=hn2T[b][:].rearrange("p k t -> p (k t)"), in_=pbk[:], func=AF.Copy),
                     reads=[pbk], writes=[hn2T[b]])
                pbp = self.pb[1]
                for kc in range(2):
                    K.op("pe", lambda e: e.transpose(out=pbp[:, kc * 128:(kc + 1) * 128], in_=ptb[b][:, kc * 128:(kc + 1) * 128],
                                                     identity=self.ident[:]),
                         reads=[ptb[b], self.ident], writes=[pbp], sig=(kc == 1))
                K.op("dve", lambda e: e.tensor_copy(out=pT[b][:].rearrange("p k t -> p (k t)"), in_=pbp[:, 0:256]),
                     reads=[pbp], writes=[pT[b]])

            def B(tt):
                b = tt % 2
                r0 = tt * 128
                for c in range(2):
                    pg = self.ps[2 + c]
                    pp = self.ps[4 + c]
                    for kc in range(8):
                        K.op("pe", lambda e: e.matmul(pg[:], lhsT=hn2T[b][:, kc, :], rhs=wg[:, kc, c * 512:(c + 1) * 512],
                                                      start=(kc == 0), stop=(kc == 7)),
                             reads=[hn2T[b], wg], writes=[pg], sig=(kc == 7))
                    for kc in range(2):
                        K.op("pe", lambda e: e.matmul(pp[:], lhsT=pT[b][:, kc, :], rhs=wp[:, kc, c * 512:(c + 1) * 512],
                                                      start=(kc == 0), stop=(kc == 1)),
                             reads=[pT[b], wp], writes=[pp], sig=(kc == 1))
                    cs = slice(c * 512, (c + 1) * 512)
                    K.op("act", lambda e: e.activation(out=gs[b][:, cs], in_=pg[:], func=AF.Sigmoid),
                         reads=[pg], writes=[gs[b]])
                    K.op("dve", lambda e: e.tensor_tensor(out=gs[b][:, cs], in0=gs[b][:, cs], in1=pp[:], op=ALU.mult),
                         reads=[gs[b], pp], writes=[gs[b]])
                    K.op("pool", lambda e: e.tensor_tensor(out=h2[b][:, cs], in0=gs[b][:, cs], in1=h1[b][:, cs], op=ALU.add),
                         reads=[gs[b], h1[b]], writes=[h2[b]])
                K.dma(self.h[r0:r0 + 128, :], h2[b][:], h2[b], False)

            LD(0)
            LD(1)
            LD(2)
            if odd:
                TR(0)
                TR(1)
            A1(0)
            A2(0)
            for tt in range(NT):
                if tt + 3 < NT:
                    LD(tt + 3)
                if odd and tt + 2 < NT:
                    TR(tt + 2)
                if tt + 1 < NT:
                    A1(tt + 1)
                B(tt)
                if tt + 1 < NT:
                    A2(tt + 1)
            K.barrier()

    def final_phase(self, src):
        K = self.K
        with ExitStack() as ts:
            gbc = self.sb(ts, "gfin", [128, D])
            K.dma(gbc[:], self.final_norm.broadcast_to([128, D]), gbc, True)
            hx = [self.sb(ts, "fhx%d" % i, [128, 4, D]) for i in range(2)]
            ho = [self.sb(ts, "fho%d" % i, [128, 4, D]) for i in range(2)]
            junk = self.sb(ts, "fjunk", [128, D])
            ss = [self.sb(ts, "fss%d" % i, [128, 4]) for i in range(2)]
            rs = [self.sb(ts, "frs%d" % i, [128, 4]) for i in range(2)]
            for tb in range(8):
                b = tb % 2
                K.dma(hx[b][:], src[tb * 512:(tb + 1) * 512, :].rearrange("(j p) d -> p j d", p=128), hx[b], True)
                for jj in range(4):
                    K.op("act", lambda e: e.activation(out=junk[:], in_=hx[b][:, jj, :], func=AF.Square,
                                                       accum_out=ss[b][:, jj:jj + 1]),
                         reads=[hx[b]], writes=[junk, ss[b]])
                self.rms_scale(ss[b], rs[b], 4, D)
                for jj in range(4):
                    K.op("dve", lambda e: e.scalar_tensor_tensor(out=ho[b][:, jj, :], in0=hx[b][:, jj, :],
                                                               scalar=rs[b][:, jj:jj + 1], in1=gbc[:],
                                                               op0=ALU.mult, op1=ALU.mult),
                         reads=[hx[b], rs[b], gbc], writes=[ho[b]])
                K.dma(self.out[tb * 512:(tb + 1) * 512, :].rearrange("(j p) d -> p j d", p=128), ho[b][:], ho[b], False)
            K.barrier()


def _consts():
    ident = np.eye(128, dtype=np.float32)
    jj = np.arange(128)[:, None]
    ss_ = np.arange(128)[None, :]
    ntinc = np.where(jj >= ss_, -1.0, 0.0).astype(np.float32)
    k = np.arange(128)[:, None, None]
    d = np.arange(4)[None, :, None]
    q = np.arange(512)[None, None, :]
    mask_s = np.where(d * 128 + k < q, 0.0, NEG).astype(np.float32)
    d2 = np.arange(2)[None, :, None]
    q2 = np.arange(256)[None, None, :]
    mask_i = np.where(d2 * 128 + k <= q2, 0.0, NEG).astype(np.float32)
    pidx = np.arange(128)
    freq = np.zeros((128, 2), np.float32)
    freq[:, 0] = (10000.0 ** (-(2.0 * (pidx % 32)) / 64.0)).astype(np.float32)
    freq[:, 1] = np.where((pidx % 64) < 32, -1.0, 1.0)
    perm = np.zeros((128, 128), np.float32)
    for i in range(128):
        perm[(i + 32) % 64 + 64 * (i // 64), i] = 1.0
    return dict(c_ident=ident, c_ntinc=ntinc, c_mask_s=mask_s, c_mask_i=mask_i, c_freq=freq, c_perm=perm)


def _shared_inputs(inp):
    f = lambda a: np.ascontiguousarray(np.asarray(a, dtype=np.float32))
    sh = {}
    for k in ("norm_mix", "norm_ple", "w_ple_gate", "w_ple_proj", "w_in_e", "lru_wa", "lru_wx", "w_out_e",
              "w_in_o", "w_out_o"):
        sh[k] = f(inp[k])
    sh["final_norm"] = f(inp["final_norm"]).reshape(1, D)
    pv = np.zeros((2, 128, 4, 8), np.float32)
    for j in range(2):
        for kk in range(4):
            pv[j, :, :, kk] = f(inp["conv_w"])[j, kk].reshape(4, 128).T
        pv[j, :, :, 4] = f(inp["conv_b"])[j].reshape(4, 128).T
        pv[j, :, :, 5] = f(inp["lru_ba"])[j].reshape(4, 128).T
        pv[j, :, :, 6] = f(inp["lru_bx"])[j].reshape(4, 128).T
        pv[j, :, :, 7] = f(inp["lru_lambda"])[j].reshape(4, 128).T
    sh["pvec_e"] = pv
    sh["lamv"] = np.ascontiguousarray(np.concatenate(
        [f(inp["lam_q1"]), f(inp["lam_k1"]), f(inp["lam_q2"]), f(inp["lam_k2"])], axis=1).reshape(2, 1, 256))
    sh["subg"] = np.ascontiguousarray(f(inp["subln_g"]).T)
    sh.update(_consts())
    return sh


_PROG_CACHE = {}


def _get_prog(n_layers=DEPTH, dbg=False):
    key = (n_layers, dbg)
    if key not in _PROG_CACHE:
        _PROG_CACHE[key] = Prog(n_layers, dbg).build()
    return _PROG_CACHE[key]


def kernel(x, p, positions, norm_mix, norm_ple, w_ple_gate, w_ple_proj, w_in_e, conv_w, conv_b, lru_wa, lru_ba,
           lru_wx, lru_bx, lru_lambda, w_out_e, w_in_o, lam_q1, lam_k1, lam_q2, lam_k2, subln_g, w_out_o,
           final_norm):
    inp = dict(norm_mix=norm_mix, norm_ple=norm_ple, w_ple_gate=w_ple_gate, w_ple_proj=w_ple_proj, w_in_e=w_in_e,
               conv_w=conv_w, conv_b=conv_b, lru_wa=lru_wa, lru_ba=lru_ba, lru_wx=lru_wx, lru_bx=lru_bx,
               lru_lambda=lru_lambda, w_out_e=w_out_e, w_in_o=w_in_o, lam_q1=lam_q1, lam_k1=lam_k1, lam_q2=lam_q2,
               lam_k2=lam_k2, subln_g=subln_g, w_out_o=w_out_o, final_norm=final_norm)
    sh = _shared_inputs(inp)
    x = np.asarray(x, dtype=np.float32)
    p = np.asarray(p, dtype=np.float32)
    positions = np.asarray(positions).astype(np.int32)
    nb = x.shape[0]
    nc = _get_prog()
    in_maps = []
    for c in range(nb):
        m = dict(sh)
        m["x"] = np.ascontiguousarray(x[c])
        m["p"] = np.ascontiguousarray(p[:, c])
        m["pos"] = np.ascontiguousarray(positions[c].reshape(1, S))
        in_maps.append(m)
    res = run_bass_kernel_spmd(nc, in_maps, core_ids=list(range(nb)))
    return np.stack([np.asarray(r["out"], dtype=np.float32) for r in res.results], axis=0)
```
